# Optimizing a Trainium2 kernel written in Bass

```python
import jax, jax.numpy as jnp
from jax import lax
import numpy as np

D_MODEL = 2048
BATCH = 4
SEQ = 2048
DEPTH = 1
DEC_BATCH = 128
DEC_SEQ = 8
PAST_LEN = 16384
PAGE_SIZE = 128

EXPAND = 2
D_MIX = EXPAND * D_MODEL
W_CONV = D_MIX // 2
W_HGRN = D_MIX - W_CONV
CONV_GROUPS = 16
CONV_WIDTH = 3
HGRN_DK = 128
HGRN_HEADS = W_HGRN // HGRN_DK
HGRN_DV = W_HGRN // HGRN_HEADS
CHUNK = 64
N_PROJ = 4 * W_CONV + 4 * W_HGRN
DEEPNORM_ALPHA = (2.0 * DEPTH) ** 0.25
DEEPNORM_BETA = (8.0 * DEPTH) ** -0.25
EPS = 1e-5

kernel_name = "hymba_conv_hgrn2_deepnorm_step"

_SPLITS = [W_CONV, 2 * W_CONV, 3 * W_CONV, 4 * W_CONV,
           4 * W_CONV + W_HGRN, 4 * W_CONV + 2 * W_HGRN, 4 * W_CONV + 3 * W_HGRN]


def _group_rms(x, gain, n_groups):
    shp = x.shape
    xf = x.astype(jnp.float32).reshape(shp[:-1] + (n_groups, shp[-1] // n_groups))
    xf = xf * lax.rsqrt(jnp.mean(xf * xf, axis=-1, keepdims=True) + EPS)
    return xf.reshape(shp) * gain.astype(jnp.float32)


def _layer_norm(x, gain, bias):
    xf = x.astype(jnp.float32)
    mu = jnp.mean(xf, axis=-1, keepdims=True)
    xc = xf - mu
    var = jnp.mean(xc * xc, axis=-1, keepdims=True)
    return xc * lax.rsqrt(var + EPS) * gain.astype(jnp.float32) + bias.astype(jnp.float32)


def _short_conv(u, buf, w):
    T = u.shape[1]
    ext = jnp.concatenate([buf.astype(u.dtype), u], axis=1)
    y = sum(ext[:, j:j + T] * w[j] for j in range(CONV_WIDTH))
    new_buf = ext[:, ext.shape[1] - (CONV_WIDTH - 1):]
    return y, new_buf


def _hgrn2_chunked(q, k, v, g, s0):
    bsz, T = q.shape[0], q.shape[1]
    n_c = -(-T // CHUNK)
    pad = n_c * CHUNK - T
    if pad:
        pw = ((0, 0), (0, pad), (0, 0), (0, 0))
        q, k, v, g = (jnp.pad(a, pw) for a in (q, k, v, g))
    rs = lambda a: a.reshape(bsz, n_c, CHUNK, HGRN_HEADS, a.shape[-1])
    q, k, v, g = rs(q), rs(k), rs(v), rs(g)
    b = jnp.cumsum(g, axis=2)
    b_last = b[:, :, -1]
    q_e = q * jnp.exp(b)
    k_e = k * jnp.exp(-b)
    k_t = k * jnp.exp(b_last[:, :, None] - b)
    scores = jnp.einsum('bnthk,bnshk->bnhts', q_e, k_e)
    causal = jnp.tril(jnp.ones((CHUNK, CHUNK), dtype=bool))
    scores = jnp.where(causal, scores, 0.0)
    o_intra = jnp.einsum('bnhts,bnshv->bnthv', scores, v)

    def step(S, xs):
        qe_c, kt_c, v_c, dl_c = xs
        o = jnp.einsum('bthk,bhkv->bthv', qe_c, S)
        S = dl_c[..., None] * S + jnp.einsum('bshk,bshv->bhkv', kt_c, v_c)
        return S, o

    xs = (jnp.moveaxis(q_e, 1, 0), jnp.moveaxis(k_t, 1, 0),
          jnp.moveaxis(v, 1, 0), jnp.moveaxis(jnp.exp(b_last), 1, 0))
    s_fin, o_inter = lax.scan(step, s0.astype(jnp.float32), xs)
    o = o_intra + jnp.moveaxis(o_inter, 0, 1)
    o = o.reshape(bsz, n_c * CHUNK, HGRN_HEADS, HGRN_DV)[:, :T]
    return o, s_fin


def _mixer_layer(x, conv_buf, hgrn_s, w_in, conv_w, norm_a, lb, norm_b, w_out, ln_g, ln_b):
    bsz, T, _ = x.shape
    proj = jnp.einsum('btd,de->bte', x, w_in)
    v_a, b_a, c_a, z_a, q_b, f_b, i_b, z_b = jnp.split(proj, _SPLITS, axis=-1)

    conv_out, new_buf = _short_conv(c_a * v_a, conv_buf, conv_w)
    y_a = _group_rms(b_a * conv_out, norm_a, CONV_GROUPS) * jax.nn.silu(z_a.astype(jnp.float32))

    hs = lambda a: a.astype(jnp.float32).reshape(bsz, T, HGRN_HEADS, -1)
    q = jax.nn.silu(hs(q_b)) * (HGRN_DK ** -0.5)
    f = lb + (1.0 - lb) * jax.nn.sigmoid(hs(f_b))
    o, new_s = _hgrn2_chunked(q, 1.0 - f, hs(i_b), jnp.log(f), hgrn_s)
    y_b = _group_rms(o.reshape(bsz, T, W_HGRN), norm_b, HGRN_HEADS) * jax.nn.silu(z_b.astype(jnp.float32))

    mix = jnp.concatenate([y_a, y_b], axis=-1).astype(x.dtype)
    h = jnp.einsum('bte,ed->btd', mix, w_out)
    y = _layer_norm(DEEPNORM_ALPHA * x.astype(jnp.float32) + h.astype(jnp.float32), ln_g, ln_b)
    return y.astype(x.dtype), new_buf.astype(conv_buf.dtype), new_s.astype(hgrn_s.dtype)


def setup_inputs(seed: int = 0) -> dict:
    key = jax.random.key(seed)
    ks = jax.random.split(key, 12)
    x_prompt = jax.random.normal(ks[0], (BATCH, SEQ, D_MODEL), jnp.float32)
    x_sample = jax.random.normal(ks[1], (DEC_BATCH, DEC_SEQ, D_MODEL), jnp.float32)
    state_conv = 0.3 * jax.random.normal(ks[2], (DEPTH, DEC_BATCH, CONV_WIDTH - 1, W_CONV), jnp.float32)
    state_hgrn = 0.1 * jax.random.normal(ks[3], (DEPTH, DEC_BATCH, HGRN_HEADS, HGRN_DK, HGRN_DV), jnp.float32)
    col_scale = jnp.concatenate([
        jnp.full((W_CONV,), DEEPNORM_BETA, jnp.float32), jnp.ones((3 * W_CONV,), jnp.float32),
        jnp.ones((2 * W_HGRN,), jnp.float32), jnp.full((W_HGRN,), DEEPNORM_BETA, jnp.float32),
        jnp.ones((W_HGRN,), jnp.float32)])
    w_in = jax.random.normal(ks[4], (DEPTH, D_MODEL, N_PROJ), jnp.float32) * (D_MODEL ** -0.5) * col_scale
    conv_w = jax.random.normal(ks[5], (DEPTH, CONV_WIDTH, W_CONV), jnp.float32) * (CONV_WIDTH ** -0.5)
    norm_a = 1.0 + 0.01 * jax.random.normal(ks[6], (DEPTH, W_CONV), jnp.float32)
    lb_logits = 0.1 * jax.random.normal(ks[7], (DEPTH + 1, W_HGRN), jnp.float32)
    norm_b = 1.0 + 0.01 * jax.random.normal(ks[8], (DEPTH, W_HGRN), jnp.float32)
    w_out = jax.random.normal(ks[9], (DEPTH, D_MIX, D_MODEL), jnp.float32) * (D_MIX ** -0.5) * DEEPNORM_BETA
    ln_gain = 1.0 + 0.01 * jax.random.normal(ks[10], (DEPTH, D_MODEL), jnp.float32)
    ln_bias = 0.01 * jax.random.normal(ks[11], (DEPTH, D_MODEL), jnp.float32)
    return {"x_prompt": x_prompt, "x_sample": x_sample, "state_conv": state_conv,
            "state_hgrn": state_hgrn, "w_in": w_in, "conv_w": conv_w, "norm_a": norm_a,
            "lb_logits": lb_logits, "norm_b": norm_b, "w_out": w_out,
            "ln_gain": ln_gain, "ln_bias": ln_bias}


def reference(x_prompt, x_sample, state_conv, state_hgrn, w_in, conv_w, norm_a, lb_logits,
              norm_b, w_out, ln_gain, ln_bias):
    lb_all = jnp.cumsum(jax.nn.softmax(lb_logits.astype(jnp.float32), axis=0), axis=0)
    yp, ys = x_prompt, x_sample
    conv_p_list, hgrn_p_list, conv_s_list, hgrn_s_list = [], [], [], []
    for l in range(DEPTH):
        lb = lb_all[l].reshape(HGRN_HEADS, HGRN_DK)
        params = (w_in[l], conv_w[l], norm_a[l], lb, norm_b[l], w_out[l], ln_gain[l], ln_bias[l])
        zero_buf = jnp.zeros((BATCH, CONV_WIDTH - 1, W_CONV), state_conv.dtype)
        zero_s = jnp.zeros((BATCH, HGRN_HEADS, HGRN_DK, HGRN_DV), state_hgrn.dtype)
        yp, cb_p, s_p = _mixer_layer(yp, zero_buf, zero_s, *params)
        ys, cb_s, s_s = _mixer_layer(ys, state_conv[l], state_hgrn[l], *params)
        conv_p_list.append(cb_p)
        hgrn_p_list.append(s_p)
        conv_s_list.append(cb_s)
        hgrn_s_list.append(s_s)
    new_conv_prompt = jnp.stack(conv_p_list, axis=0)
    new_hgrn_prompt = jnp.stack(hgrn_p_list, axis=0)
    new_conv_sample = jnp.stack(conv_s_list, axis=0)
    new_hgrn_sample = jnp.stack(hgrn_s_list, axis=0)
    return (yp, ys, new_conv_prompt, new_hgrn_prompt, new_conv_sample, new_hgrn_sample)
```

```python
import numpy as np
from contextlib import ExitStack
import concourse.bass as bass
import concourse.mybir as mybir
from concourse.bass_utils import run_bass_kernel_spmd

F32 = mybir.dt.float32
BF16 = mybir.dt.bfloat16
AF = mybir.ActivationFunctionType
ALU = mybir.AluOpType

D_MODEL = 2048
NCORES = 8
NPR = 1024
NSM = 128
NTOK = NPR + NSM
XC = NTOK + 2
EPS = 1e-5
QSCALE = 128.0 ** -0.5
ALPHA = 2.0 ** 0.25
TW = 388


class _Op:
    __slots__ = ("idx", "eng", "fn", "deps", "semkey", "signal", "ticket", "has_dep")

    def __init__(self, idx, eng, fn, deps, semkey):
        self.idx = idx
        self.eng = eng
        self.fn = fn
        self.deps = deps
        self.semkey = semkey
        self.signal = None
        self.ticket = None
        self.has_dep = False


class Sched:
    ENGS = ("pe", "act", "dve", "pool", "sp")

    def __init__(self, nc):
        self.nc = nc
        self.ops = []
        self.last_writer = {}
        self.readers = {}
        self.final_dma = []
        self.floor = None

    def add(self, eng, fn, reads=(), writes=(), semkey=None, final=False):
        deps = set()
        if self.floor is not None:
            deps.add(self.floor)
        for k in reads:
            w = self.last_writer.get(k)
            if w is not None:
                deps.add(w)
        for k in writes:
            w = self.last_writer.get(k)
            if w is not None:
                deps.add(w)
            for r in self.readers.get(k, ()):
                deps.add(r)
        idx = len(self.ops)
        op = _Op(idx, eng, fn, deps, semkey)
        self.ops.append(op)
        for k in reads:
            self.readers.setdefault(k, []).append(idx)
        for k in writes:
            self.last_writer[k] = idx
            self.readers[k] = []
        if final:
            self.final_dma.append(idx)
        return idx

    def barrier(self, eng, fn):
        deps = set()
        if self.floor is not None:
            deps.add(self.floor)
        for w in self.last_writer.values():
            deps.add(w)
        for rs in self.readers.values():
            deps.update(rs)
        idx = len(self.ops)
        op = _Op(idx, eng, fn, deps, None)
        self.ops.append(op)
        self.floor = idx
        self.last_writer = {}
        self.readers = {}
        return idx

    def finalize(self, stack):
        nc = self.nc
        ops = self.ops
        for op in ops:
            for d in op.deps:
                dop = ops[d]
                if dop.eng == "pe" and op.eng == "pe" and dop.semkey is None:
                    continue
                dop.has_dep = True
        for i in self.final_dma:
            ops[i].has_dep = True
        self.eng_sem = {}
        for e in self.ENGS:
            self.eng_sem[e] = stack.enter_context(nc.semaphore("s_" + e))
        self.dma_sem = {}
        cnt = {e: 0 for e in self.ENGS}
        dcnt = {}
        for op in ops:
            if op.semkey is not None:
                if op.semkey not in self.dma_sem:
                    self.dma_sem[op.semkey] = stack.enter_context(
                        nc.semaphore("d_" + str(op.semkey)))
                    dcnt[op.semkey] = 0
                dcnt[op.semkey] += 16
                op.signal = self.dma_sem[op.semkey]
                op.ticket = dcnt[op.semkey]
            elif op.has_dep:
                cnt[op.eng] += 1
                op.signal = self.eng_sem[op.eng]
                op.ticket = cnt[op.eng]
        self.n_sig = cnt

    def emit(self, eng, e):
        ops = self.ops
        waited = {}
        for op in ops:
            if op.eng != eng:
                continue
            need = {}
            for d in op.deps:
                dop = ops[d]
                if dop.signal is None:
                    continue
                if dop.eng == "pe" and eng == "pe" and dop.semkey is None:
                    continue
                key = id(dop.signal)
                if key not in need or need[key][1] < dop.ticket:
                    need[key] = (dop.signal, dop.ticket)
            for key, (sem, val) in need.items():
                if waited.get(key, 0) >= val:
                    continue
                e.wait_ge(sem, val)
                waited[key] = val
            ins = op.fn(e)
            if op.semkey is not None:
                ins.then_inc(op.signal, 16)
            elif op.signal is not None:
                ins.then_inc(op.signal, 1)
        if eng == "sp":
            need = {}
            for i in self.final_dma:
                dop = ops[i]
                key = id(dop.signal)
                if key not in need or need[key][1] < dop.ticket:
                    need[key] = (dop.signal, dop.ticket)
            for key, (sem, val) in need.items():
                e.wait_ge(sem, val)


def _interleave(a, b, lead=0):
    ia, ib = 0, 0
    while ia < min(lead, len(a)):
        a[ia]()
        ia += 1
    while ia < len(a) or ib < len(b):
        if ib < len(b):
            b[ib]()
            ib += 1
        if ia < len(a):
            a[ia]()
            ia += 1


class _Blk:
    def __init__(self):
        self.pgroups = []
        self.pre = None
        self.pieces = []
        self.fin = None


def build():
    nc = bass.Bass("TRN2", target_bir_lowering=False)

    def din(name, shape):
        return nc.dram_tensor(name, shape, F32, kind="ExternalInput").ap()

    def dout(name, shape):
        return nc.dram_tensor(name, shape, F32, kind="ExternalOutput").ap()

    xm_d = din("xm", [NTOK, D_MODEL])
    xp_d = din("xp", [NPR, D_MODEL])
    wu_d = din("wu", [D_MODEL, 32 * 512])
    wo_d = din("wo", [4096, D_MODEL])
    cw_d = din("cw", [128, 48])
    na_d = din("na", [128, 16])
    nb_d = din("nb", [128, 16])
    lbl_d = din("lbl", [128, 32])
    lng_d = din("lng", [128, D_MODEL])
    lnb_d = din("lnb", [128, D_MODEL])
    scv_d = din("scv", [32, D_MODEL])
    shg_d = din("shg", [16, 16, 128, 128])
    idf_d = din("identf", [128, 128])
    mkp_d = din("maskp", [128, 128])
    mks_d = din("masks", [128, 128])
    mk64_d = din("mk64", [128, 385])
    mkb_d = din("mkb", [128, 385])
    mr64_d = din("mr64", [128, 385])
    mrb_d = din("mrb", [128, 385])
    sel_d = din("selcol", [128, 16])
    ones_d = din("onesdiv", [128, 128])

    y_d = dout("y", [NTOK, D_MODEL])
    ncv_d = dout("ncv", [34, D_MODEL])
    nhp_d = dout("nhp", [16, 128, 128])
    nhs_d = dout("nhs", [16, 16, 128, 128])

    wu_v = wu_d.rearrange("(kc p) c -> p kc c", p=128)
    wo_v = wo_d.rearrange("(kc p) c -> p kc c", p=128)

    with ExitStack() as st:
        def T(name, shape, dt):
            return st.enter_context(nc.sbuf_tensor(name, shape, dt))

        def PS(name, shape, dt):
            return st.enter_context(nc.psum_tensor(name, shape, dt))

        BIGN = 126976
        BIG = T("big", [128, BIGN // 2], BF16)
        RREG = T("rreg", [128, 16 * NTOK], BF16)
        MIXB = T("mixb", [128, 16, NTOK], BF16)
        identb = T("identb", [128, 128], BF16)
        identf = T("identf_s", [128, 128], F32)
        maskp = T("maskp_s", [128, 128], F32)
        masks = T("masks_s", [128, 128], F32)
        mk64 = T("mk64_s", [128, 385], F32)
        mkb = T("mkb_s", [128, 385], F32)
        selcol = T("selcol_s", [128, 16], F32)
        selb = T("selb_s", [128, 16], BF16)
        mr64 = T("mr64_s", [128, 385], F32)
        mrb = T("mrb_s", [128, 385], F32)
        rowm = T("rowm_s", [128, 2], F32)
        hc0 = T("hc0_s", [128, 16], F32)
        hc1 = T("hc1_s", [128, 16], F32)
        hnc1 = T("hnc1_s", [128, 16], F32)
        onesdiv = T("onesdiv_s", [128, 128], F32)
        cw = T("cw_s", [128, 48], F32)
        na = T("na_s", [128, 16], F32)
        nb = T("nb_s", [128, 16], F32)
        lbl = T("lbl_s", [128, 32], F32)
        lb = T("lb_s", [128, 16], F32)
        oml = T("oml_s", [128, 16], F32)
        noml = T("noml_s", [128, 16], F32)
        BL = T("bl_s", [128, 24], F32)
        DL = T("dl_s", [128, 24], F32)
        dummy = T("dummy_s", [128, 4], F32)
        stat = T("stat_s", [128, 8], F32)
        SCG = T("scg_s", [32, 128], F32)

        xprevT = RREG[:, 0:16 * NPR].rearrange("p (a b) -> p a b", a=16)
        MIXA = RREG[:, :].rearrange("p (a b) -> p a b", a=16)

        pbank = {}
        for i in (0, 1, 2, 4, 5, 6, 7):
            pbank[i] = PS("pb%d" % i, [128, 512], F32)
        pT = PS("pT", [128, 1024], BF16)

        def carve(off, shape, dt):
            n = 1
            for d in shape[1:]:
                n *= d
            esz = 2 if dt == BF16 else 4
            assert off % 4 == 0
            v = BIG[:, off // 2: off // 2 + n * esz // 2]
            if dt == F32:
                v = v.bitcast(F32)
            if len(shape) == 3:
                v = v.rearrange("p (a b) -> p a b", a=shape[1])
            return v

        off = 0
        xT = carve(off, [128, 16, XC], BF16); off += 16 * XC * 2
        WSL = []
        for sl in range(2):
            WSL.append(carve(off, [128, 16, 512], BF16)); off += 16384
        STRAW = off; off += 8192
        XS = [carve(STRAW, [128, 2048], BF16), carve(STRAW + 4096, [128, 2048], BF16)]
        S0 = carve(STRAW, [128, 16, 128], F32)
        CVO = carve(STRAW, [128, 2048], F32)
        ET = [[None] * 3 for _ in range(2)]
        for par in range(2):
            for i in range(3):
                ET[par][i] = carve(off, [128, TW], F32); off += TW * 4
        G = []
        for i in range(7):
            G.append(carve(off, [128, TW], F32)); off += TW * 4
        QE, KE, KTC, DLT = [], [], [], []
        for par in range(2):
            QE.append(carve(off, [128, TW], BF16)); off += TW * 2
            KE.append(carve(off, [128, TW], BF16)); off += TW * 2
            KTC.append(carve(off, [128, TW], BF16)); off += TW * 2
            DLT.append(carve(off, [128, 24], F32)); off += 96
        VCM = carve(off, [128, TW], BF16); off += TW * 2
        VT = []
        for par in range(2):
            VT.append(carve(off, [128, 384], BF16)); off += 768
        KT = []
        SCM = []
        for i in range(3):
            KT.append(carve(off, [128, 128], BF16)); off += 256
        for i in range(3):
            SCM.append(carve(off, [128, 128], BF16)); off += 256
        SST = [[None, None], [None, None]]
        for i in range(2):
            for r in range(2):
                SST[i][r] = carve(off, [128, 128], F32); off += 512
        NSBF = 6
        SBF = []
        for i in range(NSBF):
            SBF.append(carve(off, [128, 128], BF16)); off += 256
        HG_OVL = off
        S0bf = carve(off, [128, 16, 128], BF16); off += 4096
        VBLK = carve(off, [128, 16, 128], BF16); off += 4096
        UB = carve(HG_OVL, [128, XC], F32)
        USM = carve(HG_OVL + XC * 4, [128, 16, 10], F32)
        UO = carve(HG_OVL + XC * 4 + 640, [128, 16, 34], F32)
        off = max(off, HG_OVL + XC * 4 + 640 + 16 * 34 * 4)
        V2 = []
        IDL = []
        for par in range(2):
            V2.append(carve(off, [128, 3 * 256], BF16)); off += 1536
            IDL.append(carve(off, [128, 24], F32)); off += 96
        assert off <= BIGN, off
        STK = [("XS", 0), ("XS", 1)]

        HB = carve(0, [128, 9, D_MODEL], F32)
        WO = [carve(73728, [128, 32, 256], BF16), carve(73728 + 16384, [128, 32, 256], BF16)]
        LNG = carve(106496, [128, D_MODEL], F32)
        LNB = carve(106496 + 8192, [128, D_MODEL], F32)

        s = Sched(nc)

        def dma(q, out, in_, reads=(), writes=(), semkey=None, final=False):
            return s.add(q, lambda e: e.dma_start(out=out, in_=in_), reads=reads, writes=writes,
                         semkey=semkey, final=final)

        def bk(i):
            return ("bank", i)

        for dst, src, key in [(identf, idf_d, "identf"), (maskp, mkp_d, "maskp"), (masks, mks_d, "masks"),
                              (mk64, mk64_d, "mk64"), (mkb, mkb_d, "mkb"), (selcol, sel_d, "selcol"),
                              (mr64, mr64_d, "mr64"), (mrb, mrb_d, "mrb"),
                              (onesdiv, ones_d, "onesdiv"), (cw, cw_d, "cw"), (na, na_d, "na"),
                              (nb, nb_d, "nb"), (lbl, lbl_d, "lbl")]:
            dma("sp", dst[:], src[:, :], writes=[key], semkey="c_" + key)
        dma("pool", identb[:], idf_d[:, :], writes=["identb"], semkey="c_identb")
        dma("pool", selb[:], sel_d[:, :], writes=["selb"], semkey="c_selb")
        s.add("dve", lambda e: e.reduce_sum(out=rowm[:, 0:1], in_=selcol[:, 0:8], axis=mybir.AxisListType.X),
              reads=["selcol"], writes=["rowm"])
        s.add("dve", lambda e: e.reduce_sum(out=rowm[:, 1:2], in_=selcol[:, 8:16], axis=mybir.AxisListType.X),
              reads=["selcol"], writes=["rowm"])
        s.add("dve", lambda e: e.memset(G[0][:], 0.5), writes=["G0"])
        s.add("dve", lambda e: e.memset(dummy[:], 0.0), writes=["dummy"])

        s.add("act", lambda e: e.activation(out=lbl[:], in_=lbl[:], func=AF.Exp), reads=["lbl"], writes=["lbl"])
        s.add("dve", lambda e: e.tensor_tensor(out=oml[:], in0=lbl[:, 0:16], in1=lbl[:, 16:32], op=ALU.add),
              reads=["lbl"], writes=["oml"])
        s.add("dve", lambda e: e.reciprocal(out=oml[:], in_=oml[:]), reads=["oml"], writes=["oml"])
        s.add("dve", lambda e: e.tensor_tensor(out=lb[:], in0=lbl[:, 0:16], in1=oml[:], op=ALU.mult),
              reads=["lbl", "oml"], writes=["lb"])
        s.add("dve", lambda e: e.tensor_scalar(out=oml[:], in0=lb[:], scalar1=-1.0, scalar2=1.0,
                                               op0=ALU.mult, op1=ALU.add), reads=["lb"], writes=["oml"])
        s.add("dve", lambda e: e.tensor_scalar(out=noml[:], in0=oml[:], scalar1=-1.0, scalar2=None,
                                               op0=ALU.mult), reads=["oml"], writes=["noml"])
        s.add("dve", lambda e: e.tensor_scalar(out=hc1[:], in0=oml[:], scalar1=0.5, scalar2=None, op0=ALU.mult),
              reads=["oml"], writes=["hc1"])
        s.add("dve", lambda e: e.tensor_scalar(out=hnc1[:], in0=oml[:], scalar1=-0.5, scalar2=None, op0=ALU.mult),
              reads=["oml"], writes=["hnc1"])
        s.add("dve", lambda e: e.tensor_tensor(out=hc0[:], in0=lb[:], in1=hc1[:], op=ALU.add),
              reads=["lb", "hc1"], writes=["hc0"])

        def load_unit_weights(u):
            sl = u % 2
            for pc in range(4):
                dma("pool", WSL[sl][:, pc * 4:(pc + 1) * 4, :], wu_v[:, pc * 4:(pc + 1) * 4, u * 512:(u + 1) * 512],
                    writes=[("w", sl, pc)], semkey="w%d_%d" % (sl, pc))

        def wkeys(sl):
            return [("w", sl, pc) for pc in range(4)]

        load_unit_weights(0)

        tcount = [0]

        XTK = [("xT", t) for t in range(9)] + ["xTh"]
        XPK = [("xp", t) for t in range(8)]

        XS4 = [XS[0], XS[1], S0bf[:, :, :].rearrange("p a b -> p (a b)"), VBLK[:, :, :].rearrange("p a b -> p (a b)")]
        XSK = [("XS", 0), ("XS", 1), "S0bf", "VBLK"]

        def load_xT(src_rows, dstT, col0, wkey):
            t = tcount[0]
            tcount[0] += 1
            par = t % 4
            xs_ap = XS4[par]
            xs_key = XSK[par]
            dma("pool", xs_ap[:], src_rows, writes=[xs_key], semkey="xs%d" % par)
            for half in range(2):
                if (2 * t + half) % 2 == 0:
                    bank_ap = pT[:, :]
                    bkey = "pT"
                else:
                    bank_ap = pbank[0][:].bitcast(BF16)
                    bkey = bk(0)
                for k8 in range(8):
                    kc = half * 8 + k8
                    s.add("pe", lambda e, kc=kc, k8=k8, bank_ap=bank_ap, xs_ap=xs_ap: e.transpose(
                        bank_ap[:, k8 * 128:(k8 + 1) * 128], xs_ap[:, kc * 128:(kc + 1) * 128], identb[:]),
                        reads=[xs_key, "identb"], writes=[bkey])
                eng = "act" if half == 0 else "dve"
                dst = dstT[:, half * 8:(half + 1) * 8, col0:col0 + 128]
                src = bank_ap.rearrange("p (a b) -> p a b", a=8)
                if eng == "act":
                    s.add("act", lambda e, dst=dst, src=src: e.copy(out=dst, in_=src), reads=[bkey], writes=[wkey])
                else:
                    s.add("dve", lambda e, dst=dst, src=src: e.tensor_copy(out=dst, in_=src), reads=[bkey], writes=[wkey])

        for t in range(8):
            load_xT(xp_d[t * 128:(t + 1) * 128, :], xprevT, t * 128, ("xp", t))
        s.add("dve", lambda e: e.tensor_copy(out=xT[:, :, 0:2], in_=xprevT[:, :, NPR - 2:NPR]),
              reads=XPK, writes=["xTh"])

        proj_rot = {"banks": [0, 1], "i": 0}

        def next_bank():
            b = proj_rot["banks"][proj_rot["i"] % len(proj_rot["banks"])]
            proj_rot["i"] += 1
            return b

        def proj_cm(u, j, src, c0, n, bank):
            sl = u % 2
            for kc in range(16):
                s.add("pe", lambda e, kc=kc: e.matmul(pbank[bank][:, 0:n], WSL[sl][:, kc, j * 128:(j + 1) * 128],
                                                       src[:, kc, c0:c0 + n], start=(kc == 0), stop=(kc == 15)),
                      reads=[("w", sl, kc // 4)] + (XTK if src is xT else XPK), writes=[bk(bank)])

        def proj_tm(u, j, src, c0, ntile):
            sl = u % 2
            for tau in range(ntile):
                for kc in range(16):
                    s.add("pe", lambda e, kc=kc, tau=tau: e.matmul(
                        pbank[4][:, tau * 128:(tau + 1) * 128], src[:, kc, c0 + tau * 128:c0 + (tau + 1) * 128],
                        WSL[sl][:, kc, j * 128:(j + 1) * 128], start=(kc == 0), stop=(kc == 15)),
                        reads=[("w", sl, kc // 4)] + (XTK if src is xT else XPK), writes=[bk(4)])

        DSB = [pbank[7][:, 0:128], pbank[4][:, 384:512]]
        DSK = [bk(7), bk(4)]
        blk_count = [0]
        tile_ctr = [0]
        ds_ctr = [0]
        snap_ctr = [0]
        hstates = [{"sidx": 0} for _ in range(17)]

        def make_hgrn_block(h, main, bi):
            u = h
            blk = _Blk()
            par = blk_count[0] % 2
            blk_count[0] += 1
            hp = h % 2
            if main:
                src = xT
                c0 = 2 + 384 * bi
                n = 384
                segs = [(0, 6, 64)] if bi < 2 else [(0, 4, 64), (256, 16, 8)]
                tiles = ["P", "P", "P"] if bi < 2 else ["P", "P", "S"]
                m0, m1 = (mk64, mr64) if bi < 2 else (mkb, mrb)
                tok0 = 384 * bi
            else:
                src = xprevT
                c0 = 384 * bi
                n = 384 if bi < 2 else 256
                segs = [(0, n // 64, 64)]
                tiles = ["P"] * (n // 128)
                m0, m1 = mk64, mr64
                tok0 = None
            ntile = len(tiles)
            Eth, Eq, Esz = ET[par]
            V = VT[par]
            QEp, KEp, KTCp, DLp = QE[par], KE[par], KTC[par], DLT[par]
            hs = slice(h, h + 1)
            dl_idx = {}
            chain_state = {}
            kq = ("QE", par)
            kk_ = ("KE", par)
            kkt = ("KTC", par)
            kdl = ("DL", par)
            kidl = ("IDL", par)
            IDLp = IDL[par]
            V2p = V2[par]
            hst = hstates[h]

            def g_f():
                b = next_bank()
                proj_cm(u, 1, src, c0, n, b)
                s.add("act", lambda e: e.activation(out=Eth[:, 0:n], in_=pbank[b][:, 0:n], func=AF.Tanh, scale=0.5),
                      reads=[bk(b)], writes=[("E", par, 0)])
                s.add("act", lambda e: e.activation(out=G[0][:, 0:n], in_=Eth[:, 0:n], func=AF.Identity,
                                                    scale=hc1[:, hs], bias=hc0[:, hs]),
                      reads=[("E", par, 0), "hc0", "hc1"], writes=["G0"])
                s.add("act", lambda e: e.activation(out=G[1][:, 0:n], in_=Eth[:, 0:n], func=AF.Identity,
                                                    scale=hnc1[:, hs], bias=hc1[:, hs]),
                      reads=[("E", par, 0), "hnc1", "hc1"], writes=["G1"])
                def rev(ap2d):
                    (ps_, pn_), (fs_, fn_) = ap2d.ap
                    return bass.AP(ap2d.tensor, ap2d.offset + (fn_ - 1) * fs_, [[ps_, pn_], [-fs_, fn_]])

                s.add("dve", lambda e: e.tensor_tensor(out=G[2][:, 0:n], in0=G[0][:, 1:n + 1], in1=m0[:, 1:n + 1],
                                                       op=ALU.mult), reads=["G0", "mk64", "mkb"], writes=["G2"])
                s.add("dve", lambda e: e.tensor_tensor_scan(out=rev(G[3][:, 0:n]), data0=rev(G[2][:, 0:n]),
                                                            data1=rev(m1[:, 1:n + 1]), initial=0.0,
                                                            op0=ALU.mult, op1=ALU.add),
                      reads=["G2", "mr64", "mrb"], writes=["G3"])
                if main:
                    s.add("dve", lambda e: e.tensor_tensor(out=G[2][:, 0:n], in0=G[1][:, 0:n], in1=G[3][:, 0:n],
                                                           op=ALU.mult), reads=["G1", "G3"], writes=["G2"])
                else:
                    s.add("dve", lambda e: e.tensor_tensor(out=KTCp[:, 0:n], in0=G[1][:, 0:n], in1=G[3][:, 0:n],
                                                           op=ALU.mult), reads=["G1", "G3"], writes=[kkt])
                o = 0
                for (sc0, nch, L) in segs:
                    fv = G[0][:, sc0:sc0 + nch * L].rearrange("p (c t) -> p c t", t=L)[:, :, 0]
                    rv = G[3][:, sc0:sc0 + nch * L].rearrange("p (c t) -> p c t", t=L)[:, :, 0]
                    s.add("dve", lambda e, fv=fv, rv=rv, o=o, nch=nch: e.tensor_tensor(
                        out=DLp[:, o:o + nch], in0=fv, in1=rv, op=ALU.mult), reads=["G0", "G3"], writes=[kdl])
                    for ci in range(nch):
                        dl_idx[(sc0, ci)] = o + ci
                    o += nch
                chain_state['ntot'] = o

            def g_q():
                b = next_bank()
                proj_cm(u, 0, src, c0, n, b)
                s.add("act", lambda e: e.activation(out=Eq[:, 0:n], in_=pbank[b][:, 0:n], func=AF.Silu),
                      reads=[bk(b)], writes=[("E", par, 1)])
                ntot = chain_state['ntot']
                if main:
                    s.add("act", lambda e: e.copy(out=KTCp[:, 0:n], in_=G[2][:, 0:n]), reads=["G2"], writes=[kkt])
                    s.add("dve", lambda e: e.reciprocal(out=IDLp[:, 0:ntot], in_=DLp[:, 0:ntot]),
                          reads=[kdl], writes=[kidl])
                    o = 0
                    for (sc0, nch, L) in segs:
                        src3 = G[2][:, sc0:sc0 + nch * L].rearrange("p (c t) -> p c t", t=L)
                        dst3 = KEp[:, sc0:sc0 + nch * L].rearrange("p (c t) -> p c t", t=L)
                        dlb = IDLp[:, o:o + nch].unsqueeze(2).to_broadcast([128, nch, L])
                        s.add("dve", lambda e, src3=src3, dst3=dst3, dlb=dlb: e.tensor_tensor(
                            out=dst3, in0=src3, in1=dlb, op=ALU.mult), reads=["G2", kidl], writes=[kk_])
                        o += nch

                if main:
                    s.add("dve", lambda e: e.tensor_tensor(out=G[3][:, 0:n], in0=G[0][:, 0:n], in1=m0[:, 0:n],
                                                           op=ALU.mult), reads=["G0", "mk64", "mkb"], writes=["G3"])
                    s.add("dve", lambda e: e.tensor_tensor(out=G[2][:, 0:n], in0=G[0][:, 0:n], in1=m1[:, 0:n],
                                                           op=ALU.mult), reads=["G0", "mr64", "mrb"], writes=["G2"])
                    s.add("dve", lambda e: e.tensor_tensor_scan(out=G[4][:, 0:n], data0=G[3][:, 0:n],
                                                                data1=G[2][:, 0:n], initial=0.0,
                                                                op0=ALU.mult, op1=ALU.add),
                          reads=["G2", "G3"], writes=["G4"])
                s.add("dve", lambda e: e.scalar_tensor_tensor(out=QEp[:, 0:n], in0=Eq[:, 0:n], scalar=QSCALE,
                                                              in1=G[4][:, 0:n], op0=ALU.mult, op1=ALU.mult),
                      reads=[("E", par, 1), "G4"], writes=[kq])

            def g_v():
                b = next_bank()
                proj_cm(u, 2, src, c0, n, b)
                s.add("act", lambda e: e.copy(out=VCM[:, 0:n], in_=pbank[b][:, 0:n]), reads=[bk(b)], writes=["VCM"])

            def g_vt():
                aux16 = pbank[4][:].bitcast(BF16)
                for tau in range(ntile):
                    s.add("pe", lambda e, tau=tau: e.transpose(aux16[:, tau * 128:(tau + 1) * 128],
                                                               VCM[:, tau * 128:(tau + 1) * 128], identb[:]),
                          reads=["VCM", "identb"], writes=[bk(4)])
                s.add("act", lambda e: e.copy(out=V[:, 0:n], in_=aux16[:, 0:n]), reads=[bk(4)], writes=[("V", par)])
                if main and bi == 2:
                    s.add("dve", lambda e: e.tensor_tensor(
                        out=VBLK[:, :, :], in0=V[:, 256:384].unsqueeze(1).to_broadcast([128, 16, 128]),
                        in1=selb[:, :].unsqueeze(2).to_broadcast([128, 16, 128]), op=ALU.mult),
                        reads=[("V", par), "selb"], writes=["VBLK"])
                for c in range(2):
                    s.add("act", lambda e, c=c: e.activation(
                        out=V2p[:, 0:ntile * 256].rearrange("p (t c v) -> p t c v", c=2, v=128)[:, :, c, :],
                        in_=aux16[:, 0:n].rearrange("p (t v) -> p t v", v=128), func=AF.Identity,
                        scale=rowm[:, c:c + 1]), reads=[bk(4), "rowm"], writes=[("V2", par)])

            def g_z():
                g_vt()
                b = next_bank()
                proj_cm(u, 3, src, c0, n, b)
                s.add("act", lambda e: e.activation(out=Esz[:, 0:n], in_=pbank[b][:, 0:n], func=AF.Silu),
                      reads=[bk(b)], writes=[("E", par, 2)])

            blk.pgroups = [g_f, g_q, g_v, g_z] if main else [g_f, g_v, g_vt]

            snaps = {}
            scm_slot = {}
            kt_slot = {}

            def make_T(tau):
                def T_():
                    tc = slice(tau * 128, (tau + 1) * 128)
                    kt = tile_ctr[0] % 3
                    tile_ctr[0] += 1
                    kt_slot[tau] = kt
                    scm_slot[tau] = kt
                    s.add("pe", lambda e: e.transpose(pT[:, 0:128], KTCp[:, tc], identb[:]),
                          reads=[kkt, "identb"], writes=["pT"])
                    s.add("act", lambda e: e.copy(out=KT[kt][:], in_=pT[:, 0:128]), reads=["pT"], writes=[("KT", kt)])
                    if main:
                        s.add("pe", lambda e: e.matmul(pbank[5][:, 0:128], KEp[:, tc], QEp[:, tc], start=True, stop=True),
                              reads=[kk_, kq], writes=[bk(5)])
                        mk = maskp if tiles[tau] == "P" else masks
                        s.add("dve", lambda e: e.tensor_tensor(out=SCM[kt][:], in0=pbank[5][:, 0:128], in1=mk[:],
                                                               op=ALU.mult),
                              reads=[bk(5), "maskp", "masks"], writes=[("SCM", kt)])
                return T_

            def make_D(tau):
                def D_():
                    tc = slice(tau * 128, (tau + 1) * 128)
                    kt = kt_slot[tau]
                    if tiles[tau] == "P":
                        dsb = 7 if ds_ctr[0] % 2 == 0 else 2
                        ds_ctr[0] += 1
                        s.add("pe", lambda e: e.matmul(pbank[dsb][:, 0:256], KT[kt][:],
                                                       V2p[:, tau * 256:(tau + 1) * 256], start=True, stop=True),
                              reads=[("KT", kt), ("V2", par)], writes=[bk(dsb)])
                        for ci in range(2):
                            cur = hst["sidx"] % 2
                            nxt = (hst["sidx"] + 1) % 2
                            hst["sidx"] += 1
                            if main:
                                sb = snap_ctr[0] % NSBF
                                snap_ctr[0] += 1
                                snaps[(tau, ci)] = sb
                                s.add("dve", lambda e, sb=sb, cur=cur: e.tensor_copy(out=SBF[sb][:],
                                                                                      in_=SST[hp][cur][:]),
                                      reads=[("SST", hp, cur)], writes=[("SBF", sb)])
                            di = dl_idx[(0, tau * 2 + ci)]
                            s.add("dve", lambda e, di=di, ci=ci, cur=cur, nxt=nxt: e.scalar_tensor_tensor(
                                out=SST[hp][nxt][:], in0=SST[hp][cur][:], scalar=DLp[:, di:di + 1],
                                in1=pbank[dsb][:, ci * 128:(ci + 1) * 128], op0=ALU.mult, op1=ALU.add),
                                reads=[("SST", hp, cur), kdl, bk(dsb)], writes=[("SST", hp, nxt)])
                    else:
                        for q in range(4):
                            db = 7 if q % 2 == 0 else 5
                            s.add("pe", lambda e, q=q, db=db: e.matmul(
                                pbank[db][:, 0:512], KT[kt][:],
                                VBLK[:, :, :].rearrange("p a b -> p (a b)")[:, 512 * q:512 * (q + 1)],
                                start=True, stop=True),
                                reads=[("KT", kt), "VBLK"], writes=[bk(db)])
                            di = dl_idx[(256, 4 * q)]
                            s0q = S0[:, 4 * q:4 * q + 4, :]
                            s.add("dve", lambda e, s0q=s0q, di=di: e.tensor_tensor(
                                out=s0q, in0=s0q, in1=DLp[:, di:di + 4].unsqueeze(2).to_broadcast([128, 4, 128]),
                                op=ALU.mult), reads=STK + [kdl], writes=STK)
                            s.add("dve", lambda e, s0q=s0q, db=db: e.tensor_tensor(
                                out=s0q, in0=s0q, in1=pbank[db][:, 0:512].rearrange("p (j v) -> p j v", j=4),
                                op=ALU.add), reads=STK + [bk(db)], writes=STK)
                        dma("sp", nhs_d[:, h, :, :].rearrange("j k v -> k j v"), S0[:, :, :], reads=STK,
                            semkey="nhs", final=True)
                return D_

            def make_B(tau):
                def B_():
                    tc = slice(tau * 128, (tau + 1) * 128)
                    sc = scm_slot[tau]
                    s.add("pe", lambda e: e.matmul(pbank[6][:, 0:128], V[:, tc], SCM[sc][:], start=True, stop=False),
                          reads=[("V", par), ("SCM", sc)], writes=[bk(6)])
                    if tiles[tau] == "P":
                        for ci in range(2):
                            sb = snaps[(tau, ci)]
                            qc = slice(tau * 128 + ci * 64, tau * 128 + (ci + 1) * 64)
                            s.add("pe", lambda e, qc=qc, sb=sb, ci=ci: e.matmul(
                                pbank[6][:, ci * 64:(ci + 1) * 64], SBF[sb][:], QEp[:, qc],
                                start=False, stop=(ci == 1)),
                                reads=[("SBF", sb), kq], writes=[bk(6)])
                    else:
                        for j in range(16):
                            qc = slice(tau * 128 + 8 * j, tau * 128 + 8 * j + 8)
                            s.add("pe", lambda e, j=j, qc=qc: e.matmul(
                                pbank[6][:, 8 * j:8 * j + 8], S0bf[:, j, :], QEp[:, qc], start=False, stop=(j == 15)),
                                reads=["S0bf", kq], writes=[bk(6)])
                    s.add("act", lambda e: e.copy(out=G[5][:, tc], in_=pbank[6][:, 0:128]),
                          reads=[bk(6)], writes=["G5"])
                return B_

            nst = ntile + (2 if main else 1)
            pcs = []
            for i in range(nst):
                parts = []
                if i < ntile:
                    parts.append(make_T(i))
                if 0 <= i - 1 < ntile:
                    parts.append(make_D(i - 1))
                if main and 0 <= i - 2 < ntile:
                    parts.append(make_B(i - 2))
                pcs.append(lambda parts=parts: [p() for p in parts])
            blk.pieces = pcs

            def fin():
                s.add("act", lambda e: e.activation(out=G[6][:, 0:n], in_=G[5][:, 0:n], func=AF.Square),
                      reads=["G5"], writes=["G6"])
                s.add("pe", lambda e: e.matmul(pbank[4][:, 0:n], onesdiv[:], G[6][:, 0:n], start=True, stop=True),
                      reads=["onesdiv", "G6"], writes=[bk(4)])
                s.add("act", lambda e: e.activation(out=G[6][:, 0:n], in_=pbank[4][:, 0:n], func=AF.Ln, bias=EPS),
                      reads=[bk(4)], writes=["G6"])
                s.add("act", lambda e: e.activation(out=G[6][:, 0:n], in_=G[6][:, 0:n], func=AF.Exp, scale=-0.5),
                      reads=["G6"], writes=["G6"])
                s.add("dve", lambda e: e.tensor_tensor(out=G[5][:, 0:n], in0=G[5][:, 0:n], in1=G[6][:, 0:n], op=ALU.mult),
                      reads=["G5", "G6"], writes=["G5"])
                s.add("dve", lambda e: e.scalar_tensor_tensor(out=MIXB[:, h, tok0:tok0 + n], in0=G[5][:, 0:n],
                                                              scalar=nb[:, hs], in1=Esz[:, 0:n],
                                                              op0=ALU.mult, op1=ALU.mult),
                      reads=["G5", "nb", ("E", par, 2)], writes=[("mixb", h)])
                if bi == 2:
                    cur = hst["sidx"] % 2
                    dma("sp", nhp_d[h, :, :], SST[hp][cur][:], reads=[("SST", hp, cur)], semkey="nhp%d" % hp,
                        final=True)

            blk.fin = fin if main else None
            return blk

        def make_conv_block(g, bi):
            u = 16 + g
            blk = _Blk()
            par = blk_count[0] % 2
            blk_count[0] += 1
            c0 = 0 if bi == 0 else 2 + 384 * bi
            n = 386 if bi == 0 else 384
            t_off = 2 if bi == 0 else 0
            tok0 = 384 * bi
            a0 = 2 + tok0
            npr = 384 if bi < 2 else 256
            Ev, Eb, Esz = ET[par]

            def g_v():
                b = next_bank()
                proj_cm(u, 0, xT, c0, n, b)
                s.add("act", lambda e: e.copy(out=Ev[:, 0:n], in_=pbank[b][:, 0:n]), reads=[bk(b)],
                      writes=[("E", par, 0)])

            def g_c():
                b = next_bank()
                proj_cm(u, 2, xT, c0, n, b)
                s.add("dve", lambda e: e.tensor_tensor(out=UB[:, c0:c0 + n], in0=pbank[b][:, 0:n], in1=Ev[:, 0:n],
                                                       op=ALU.mult),
                      reads=[bk(b), ("E", par, 0)], writes=["UB"])

            def g_b():
                b = next_bank()
                proj_cm(u, 1, xT, c0, n, b)
                s.add("act", lambda e: e.copy(out=Eb[:, 0:n], in_=pbank[b][:, 0:n]), reads=[bk(b)],
                      writes=[("E", par, 1)])

            def g_z():
                b = next_bank()
                proj_cm(u, 3, xT, c0, n, b)
                s.add("act", lambda e: e.activation(out=Esz[:, 0:n], in_=pbank[b][:, 0:n], func=AF.Silu),
                      reads=[bk(b)], writes=[("E", par, 2)])

            blk.pgroups = [g_v, g_c, g_b, g_z]

            def w(jj):
                return cw[:, g * 3 + jj:g * 3 + jj + 1]

            def c1():
                YC = G[0]
                s.add("dve", lambda e: e.tensor_scalar(out=YC[:, 0:npr], in0=UB[:, a0:a0 + npr], scalar1=w(2),
                                                       scalar2=None, op0=ALU.mult),
                      reads=["UB", "cw"], writes=["G0"])
                for jj, sh in ((1, 1), (0, 2)):
                    s.add("dve", lambda e, jj=jj, sh=sh: e.scalar_tensor_tensor(
                        out=YC[:, 0:npr], in0=UB[:, a0 - sh:a0 - sh + npr], scalar=w(jj), in1=YC[:, 0:npr],
                        op0=ALU.mult, op1=ALU.add), reads=["UB", "cw", "G0"], writes=["G0"])
                if bi == 2:
                    dma("sp", SCG[:], scv_d[:, g * 128:(g + 1) * 128], writes=["SCG"], semkey="scg")
                    s.add("pe", lambda e: e.transpose(pbank[4][:, 0:32], SCG[:], identf[0:32, 0:32]),
                          reads=["SCG", "identf"], writes=[bk(4)])
                    s.add("dve", lambda e: e.tensor_copy(
                        out=USM[:, :, 0:2], in_=pbank[4][:, 0:32].rearrange("p (j r) -> p j r", r=2)),
                        reads=[bk(4)], writes=["USM"])
                    s.add("dve", lambda e: e.tensor_copy(
                        out=USM[:, :, 2:10], in_=UB[:, 2 + NPR:2 + NTOK].rearrange("p (j t) -> p j t", t=8)),
                        reads=["UB"], writes=["USM"])
                    YS = YC[:, 256:384].rearrange("p (j t) -> p j t", t=8)
                    s.add("dve", lambda e: e.tensor_scalar(out=YS, in0=USM[:, :, 2:10], scalar1=w(2), scalar2=None,
                                                           op0=ALU.mult), reads=["USM", "cw"], writes=["G0"])
                    for jj, sh in ((1, 1), (0, 2)):
                        s.add("dve", lambda e, jj=jj, sh=sh: e.scalar_tensor_tensor(
                            out=YS, in0=USM[:, :, 2 - sh:10 - sh], scalar=w(jj), in1=YS,
                            op0=ALU.mult, op1=ALU.add), reads=["USM", "cw", "G0"], writes=["G0"])
                    s.add("dve", lambda e: e.tensor_copy(out=UO[:, g, 0:2], in_=UB[:, NPR:NPR + 2]),
                          reads=["UB"], writes=["UO"])
                    s.add("dve", lambda e: e.tensor_copy(
                        out=UO[:, g, 2:34].rearrange("p (j r) -> p j r", r=2), in_=USM[:, :, 8:10]),
                        reads=["USM"], writes=["UO"])
                s.add("dve", lambda e: e.tensor_tensor(out=G[1][:, 0:384], in0=Eb[:, t_off:t_off + 384],
                                                       in1=YC[:, 0:384], op=ALU.mult),
                      reads=[("E", par, 1), "G0"], writes=["G1"])
                s.add("act", lambda e: e.activation(out=G[2][:, 0:384], in_=G[1][:, 0:384], func=AF.Square),
                      reads=["G1"], writes=["G2"])

            def c2():
                gs = slice(g, g + 1)
                s.add("pe", lambda e: e.matmul(pbank[4][:, 0:384], onesdiv[:], G[2][:, 0:384], start=True, stop=True),
                      reads=["onesdiv", "G2"], writes=[bk(4)])
                s.add("act", lambda e: e.activation(out=G[2][:, 0:384], in_=pbank[4][:, 0:384], func=AF.Ln, bias=EPS),
                      reads=[bk(4)], writes=["G2"])
                s.add("act", lambda e: e.activation(out=G[2][:, 0:384], in_=G[2][:, 0:384], func=AF.Exp, scale=-0.5),
                      reads=["G2"], writes=["G2"])
                s.add("dve", lambda e: e.tensor_tensor(out=G[1][:, 0:384], in0=G[1][:, 0:384], in1=G[2][:, 0:384],
                                                       op=ALU.mult), reads=["G1", "G2"], writes=["G1"])
                s.add("dve", lambda e: e.scalar_tensor_tensor(out=MIXA[:, g, tok0:tok0 + 384], in0=G[1][:, 0:384],
                                                              scalar=na[:, gs], in1=Esz[:, t_off:t_off + 384],
                                                              op0=ALU.mult, op1=ALU.mult),
                      reads=["G1", "na", ("E", par, 2)], writes=[("mixa", g)])

            blk.pieces = [c1, c2]
            return blk

        LEAD = 0

        def run_block(blk, prev, init=None):
            _interleave(blk.pgroups, prev.pieces if prev is not None else [],
                        lead=(0 if len(blk.pgroups) >= 4 else 1))
            if prev is not None and prev.fin is not None:
                prev.fin()
            if init is not None:
                init()
            return blk

        def run_blocks(blocks, prev):
            for blk in blocks:
                prev = run_block(blk, prev)
            return prev

        def init_state(h):
            hp = h % 2
            hstates[h]["sidx"] = 0
            s.add("dve", lambda e: e.memset(SST[hp][0][:], 0.0), writes=[("SST", hp, 0)])

        def init_sample(h):
            dma("sp", S0[:, :, :], shg_d[:, h, :, :].rearrange("j k v -> k j v"), writes=STK, semkey="st")
            dma("pool", S0bf[:, :, :], shg_d[:, h, :, :].rearrange("j k v -> k j v"), writes=["S0bf"], semkey="stb")

        load_unit_weights(1)
        prev = None
        init_state(0)
        for bi in range(3):
            prev = run_block(make_hgrn_block(0, False, bi), prev, (lambda: init_sample(0)) if bi == 2 else None)
            if bi < 2:
                for t in (range(0, 5) if bi == 0 else range(5, 9)):
                    load_xT(xm_d[t * 128:(t + 1) * 128, :], xT, 2 + t * 128, ("xT", t))
        for h in range(16):
            nh = h + 1
            prev = run_block(make_hgrn_block(h, True, 0), prev)
            prev = run_block(make_hgrn_block(h, True, 1), prev)
            if nh < 16:
                init_state(nh)
                prev = run_block(make_hgrn_block(nh, False, 0), prev)
            prev = run_block(make_hgrn_block(h, True, 2), prev)
            load_unit_weights(h + 2)
            if nh < 16:
                prev = run_block(make_hgrn_block(nh, False, 1), prev, lambda nh=nh: init_sample(nh))
                prev = run_block(make_hgrn_block(nh, False, 2), prev)
        _interleave([], prev.pieces)
        if prev.fin is not None:
            prev.fin()
        prev = None
        s.add("dve", lambda e: e.memset(dummy[:, 0:1], 0.0),
              writes=["dummy", "S0bf", "VBLK", "UB", "USM", "UO"] + XPK + [("mixa", g) for g in range(16)])
        proj_rot["banks"] = [0, 1, 2, 5, 6, 7]
        for g in range(16):
            u = 16 + g
            if g >= 1 and u + 1 < 32:
                load_unit_weights(u + 1)
            blocks = [make_conv_block(g, bi) for bi in range(3)]
            prev = run_blocks(blocks, prev)
        _interleave([], prev.pieces)

        for g in range(16):
            b = 5 + (g // 4) % 2
            s.add("pe", lambda e, g=g, b=b: e.transpose(pbank[b][0:34, (g % 4) * 128:(g % 4 + 1) * 128],
                                                        UO[:, g, :], identf[:]),
                  reads=["UO", "identf"], writes=[bk(b)])
            if g % 4 == 3:
                gq = g // 4
                s.add("dve", lambda e, gq=gq, b=b: e.tensor_copy(out=CVO[0:34, gq * 512:(gq + 1) * 512],
                                                                 in_=pbank[b][0:34, :]),
                      reads=[bk(b)], writes=STK)
        dma("sp", ncv_d[:, :], CVO[0:34, :], reads=STK, semkey="ncv", final=True)

        s.barrier("dve", lambda e: e.memset(dummy[:, 3:4], 0.0))
        for t in range(9):
            dma("sp", HB[:, t, :], xm_d[t * 128:(t + 1) * 128, :], writes=[("hb", t)], semkey="hb%d" % t)
        dma("sp", LNG[:], lng_d[:, :], writes=["lng"], semkey="c_lng")
        dma("sp", LNB[:], lnb_d[:, :], writes=["lnb"], semkey="c_lnb")

        def load_wo(n8):
            sl = n8 % 2
            for pc in range(4):
                dma("pool", WO[sl][:, pc * 8:(pc + 1) * 8, :], wo_v[:, pc * 8:(pc + 1) * 8, n8 * 256:(n8 + 1) * 256],
                    writes=[("wo", sl, pc)], semkey="wo%d_%d" % (sl, pc))

        stat = T("stat2_s", [128, 16], F32)
        rsum = T("rsum_s", [128, 72], F32)
        s.add("dve", lambda e: e.memset(rsum[:], 0.0), writes=["rsum"])
        JUNK = carve(73728, [128, D_MODEL], F32)
        jkeys = [("wo", 0, pc) for pc in range(4)]

        def ln_stats(t):
            hv = HB[:, t, :]
            p8 = (t % 2) * 8
            c_sum, c_nm, c_ssq, c_rstd = (stat[:, p8 + i:p8 + i + 1] for i in range(4))
            sk = ("stat", t % 2)
            s.add("dve", lambda e: e.memset(stat[:, p8:p8 + 8], 0.0), writes=[sk])
            s.add("dve", lambda e: e.reduce_sum(out=c_sum, in_=rsum[:, t * 8:(t + 1) * 8], axis=mybir.AxisListType.X),
                  reads=[("hb", t), "rsum", sk], writes=[sk])
            s.add("dve", lambda e: e.tensor_scalar(out=c_nm, in0=c_sum, scalar1=-1.0 / D_MODEL, scalar2=None,
                                                   op0=ALU.mult), reads=[sk], writes=[sk])
            s.add("act", lambda e: e.activation(out=hv, in_=hv, func=AF.Identity, bias=c_nm),
                  reads=[("hb", t), sk], writes=[("hb", t)])
            s.add("act", lambda e: e.activation(out=JUNK[:, :], in_=hv, func=AF.Square, accum_out=c_ssq),
                  reads=[("hb", t), sk], writes=[sk] + jkeys)
            s.add("act", lambda e: e.activation(out=c_rstd, in_=c_ssq, func=AF.Ln, scale=1.0 / D_MODEL, bias=EPS),
                  reads=[sk], writes=[sk])
            s.add("act", lambda e: e.activation(out=c_rstd, in_=c_rstd, func=AF.Exp, scale=-0.5),
                  reads=[sk], writes=[sk])

        def ln_apply(t):
            hv = HB[:, t, :]
            p8 = (t % 2) * 8
            c_rstd = stat[:, p8 + 3:p8 + 4]
            sk = ("stat", t % 2)
            s.add("dve", lambda e: e.scalar_tensor_tensor(out=hv, in0=hv, scalar=c_rstd, in1=LNG[:, :],
                                                          op0=ALU.mult, op1=ALU.mult),
                  reads=[sk, "lng", ("hb", t)], writes=[("hb", t)])
            s.add("dve", lambda e: e.tensor_tensor(out=hv, in0=hv, in1=LNB[:, :], op=ALU.add),
                  reads=["lnb", ("hb", t)], writes=[("hb", t)])
            dma("sp", y_d[t * 128:(t + 1) * 128, :], hv, reads=[("hb", t)], semkey="y%d" % t, final=True)

        load_wo(0)
        rot = 0
        banks2 = [0, 1, 2, 4, 5, 6, 7]
        for n8 in range(8):
            if n8 + 1 < 8:
                load_wo(n8 + 1)
            sl = n8 % 2
            for t in range(9):
                b = banks2[rot % len(banks2)]
                rot += 1
                for kc in range(32):
                    mx = MIXA[:, kc, t * 128:(t + 1) * 128] if kc < 16 else MIXB[:, kc - 16, t * 128:(t + 1) * 128]
                    s.add("pe", lambda e, kc=kc, mx=mx, b=b, sl=sl: e.matmul(pbank[b][:, 0:256], mx, WO[sl][:, kc, :],
                                                                             start=(kc == 0), stop=(kc == 31)),
                          reads=[("wo", sl, kc // 8)], writes=[bk(b)])
                hv8 = HB[:, t, n8 * 256:(n8 + 1) * 256]
                rs = rsum[:, t * 8 + n8:t * 8 + n8 + 1]
                s.add("dve", lambda e, hv8=hv8, b=b, rs=rs: e.scalar_tensor_tensor(out=hv8, in0=hv8, scalar=ALPHA,
                                                                                   in1=pbank[b][:, 0:256],
                                                                                   op0=ALU.mult, op1=ALU.add,
                                                                                   accum_out=rs),
                      reads=[bk(b), ("hb", t), "rsum"], writes=[("hb", t)])
                if n8 == 7:
                    if t >= 1:
                        ln_stats(t - 1)
                    if t >= 2:
                        ln_apply(t - 2)
        ln_stats(8)
        ln_apply(7)
        ln_apply(8)

        s.finalize(st)
        with nc.Block() as block:
            @block.tensor
            def _(e):
                s.emit("pe", e)

            @block.scalar
            def _(e):
                s.emit("act", e)

            @block.vector
            def _(e):
                s.emit("dve", e)

            @block.gpsimd
            def _(e):
                s.emit("pool", e)

            @block.sync
            def _(e):
                s.emit("sp", e)
    return nc


_NC_CACHE = {}


def _consts():
    idx = np.arange(128)
    identf = np.eye(128, dtype=np.float32)
    sidx = idx[:, None]
    tidx = idx[None, :]
    maskp = ((sidx <= tidx) & (sidx // 64 == tidx // 64)).astype(np.float32)
    masks = ((sidx <= tidx) & (sidx // 8 == tidx // 8)).astype(np.float32)
    mk64 = np.ones((128, 385), np.float32)
    mk64[:, 0::64] = 0.0
    mkb = np.ones((128, 385), np.float32)
    mkb[:, 0:256:64] = 0.0
    mkb[:, 256::8] = 0.0
    selcol = (idx[:, None] // 8 == np.arange(16)[None, :]).astype(np.float32)
    onesdiv = np.full((128, 128), 1.0 / 128.0, np.float32)
    return dict(identf=identf, maskp=maskp, masks=masks, mk64=mk64, mkb=mkb, selcol=selcol, onesdiv=onesdiv,
                mr64=(1.0 - mk64).astype(np.float32), mrb=(1.0 - mkb).astype(np.float32))


def kernel(x_prompt, x_sample, state_conv, state_hgrn, w_in, conv_w, norm_a, lb_logits, norm_b, w_out,
           ln_gain, ln_bias):
    f32 = np.float32
    x_prompt = np.asarray(x_prompt, f32)
    x_sample = np.asarray(x_sample, f32)
    state_conv = np.asarray(state_conv, f32)
    state_hgrn = np.asarray(state_hgrn, f32)
    w_in = np.asarray(w_in, f32)
    w_out = np.asarray(w_out, f32)
    if "nc" not in _NC_CACHE:
        _NC_CACHE["nc"] = build()
    nc = _NC_CACHE["nc"]

    w8 = w_in[0].reshape(D_MODEL, 8, 16, 128)
    wh = w8[:, 4:8].transpose(0, 2, 1, 3)
    wc = w8[:, 0:4].transpose(0, 2, 1, 3)
    wu = np.ascontiguousarray(np.concatenate([wh, wc], axis=1).reshape(D_MODEL, 32 * 512))
    wo = np.ascontiguousarray(w_out[0])

    def cm(vec):
        return np.ascontiguousarray(np.asarray(vec, f32).reshape(16, 128).T)

    cw = np.ascontiguousarray(np.asarray(conv_w, f32)[0].reshape(3, 16, 128).transpose(2, 1, 0).reshape(128, 48))
    na = cm(norm_a[0])
    nb = cm(norm_b[0])
    lbl = np.ascontiguousarray(np.concatenate([cm(lb_logits[0]), cm(lb_logits[1])], axis=1))
    lng = np.ascontiguousarray(np.broadcast_to(np.asarray(ln_gain, f32)[0][None, :], (128, D_MODEL)))
    lnb = np.ascontiguousarray(np.broadcast_to(np.asarray(ln_bias, f32)[0][None, :], (128, D_MODEL)))
    consts = _consts()

    in_maps = []
    for c in range(NCORES):
        b, hf = c // 2, c % 2
        xm = np.concatenate([x_prompt[b, hf * NPR:(hf + 1) * NPR],
                             x_sample[16 * c:16 * c + 16].reshape(NSM, D_MODEL)], axis=0)
        xp = x_prompt[b, 0:NPR] if hf == 1 else np.zeros((NPR, D_MODEL), f32)
        m = dict(xm=np.ascontiguousarray(xm), xp=np.ascontiguousarray(xp), wu=wu, wo=wo, cw=cw, na=na, nb=nb,
                 lbl=lbl, lng=lng, lnb=lnb,
                 scv=np.ascontiguousarray(state_conv[0, 16 * c:16 * c + 16].reshape(32, D_MODEL)),
                 shg=np.ascontiguousarray(state_hgrn[0, 16 * c:16 * c + 16]))
        m.update(consts)
        in_maps.append(m)

    res = run_bass_kernel_spmd(nc, in_maps, core_ids=list(range(NCORES)))
    R = res.results

    y_prompt = np.empty((4, 2048, D_MODEL), f32)
    y_sample = np.empty((128, 8, D_MODEL), f32)
    ncp = np.empty((1, 4, 2, 2048), f32)
    nhp = np.empty((1, 4, 16, 128, 128), f32)
    ncs = np.empty((1, 128, 2, 2048), f32)
    nhs = np.empty((1, 128, 16, 128, 128), f32)
    for c in range(NCORES):
        b, hf = c // 2, c % 2
        r = R[c]
        y_prompt[b, hf * NPR:(hf + 1) * NPR] = r["y"][0:NPR]
        y_sample[16 * c:16 * c + 16] = r["y"][NPR:NTOK].reshape(16, 8, D_MODEL)
        ncs[0, 16 * c:16 * c + 16] = r["ncv"][2:34].reshape(16, 2, 2048)
        nhs[0, 16 * c:16 * c + 16] = r["nhs"]
        if hf == 1:
            ncp[0, b] = r["ncv"][0:2]
            nhp[0, b] = r["nhp"]
    return (y_prompt, y_sample, ncp, nhp, ncs, nhs)
```

```python
import numpy as np
from contextlib import ExitStack
import concourse.bass as bass
import concourse.mybir as mybir
from concourse.bass_utils import run_bass_kernel_spmd

F32 = mybir.dt.float32
BF16 = mybir.dt.bfloat16
AF = mybir.ActivationFunctionType
ALU = mybir.AluOpType

D_MODEL = 2048
NCORES = 8
NPR = 1024
NSM = 128
NTOK = NPR + NSM
XC = NTOK + 2
EPS = 1e-5
QSCALE = 128.0 ** -0.5
ALPHA = 2.0 ** 0.25
TW = 388


class _Op:
    __slots__ = ("idx", "eng", "fn", "deps", "semkey", "signal", "ticket", "has_dep")

    def __init__(self, idx, eng, fn, deps, semkey):
        self.idx = idx
        self.eng = eng
        self.fn = fn
        self.deps = deps
        self.semkey = semkey
        self.signal = None
        self.ticket = None
        self.has_dep = False


class Sched:
    ENGS = ("pe", "act", "dve", "pool", "sp")

    def __init__(self, nc):
        self.nc = nc
        self.ops = []
        self.last_writer = {}
        self.readers = {}
        self.final_dma = []
        self.floor = None

    def add(self, eng, fn, reads=(), writes=(), semkey=None, final=False):
        deps = set()
        if self.floor is not None:
            deps.add(self.floor)
        for k in reads:
            w = self.last_writer.get(k)
            if w is not None:
                deps.add(w)
        for k in writes:
            w = self.last_writer.get(k)
            if w is not None:
                deps.add(w)
            for r in self.readers.get(k, ()):
                deps.add(r)
        idx = len(self.ops)
        op = _Op(idx, eng, fn, deps, semkey)
        self.ops.append(op)
        for k in reads:
            self.readers.setdefault(k, []).append(idx)
        for k in writes:
            self.last_writer[k] = idx
            self.readers[k] = []
        if final:
            self.final_dma.append(idx)
        return idx

    def barrier(self, eng, fn):
        deps = set()
        if self.floor is not None:
            deps.add(self.floor)
        for w in self.last_writer.values():
            deps.add(w)
        for rs in self.readers.values():
            deps.update(rs)
        idx = len(self.ops)
        op = _Op(idx, eng, fn, deps, None)
        self.ops.append(op)
        self.floor = idx
        self.last_writer = {}
        self.readers = {}
        return idx

    def finalize(self, stack):
        nc = self.nc
        ops = self.ops
        for op in ops:
            for d in op.deps:
                dop = ops[d]
                if dop.eng == "pe" and op.eng == "pe" and dop.semkey is None:
                    continue
                dop.has_dep = True
        for i in self.final_dma:
            ops[i].has_dep = True
        self.eng_sem = {}
        for e in self.ENGS:
            self.eng_sem[e] = stack.enter_context(nc.semaphore("s_" + e))
        self.dma_sem = {}
        cnt = {e: 0 for e in self.ENGS}
        dcnt = {}
        for op in ops:
            if op.semkey is not None:
                if op.semkey not in self.dma_sem:
                    self.dma_sem[op.semkey] = stack.enter_context(
                        nc.semaphore("d_" + str(op.semkey)))
                    dcnt[op.semkey] = 0
                dcnt[op.semkey] += 16
                op.signal = self.dma_sem[op.semkey]
                op.ticket = dcnt[op.semkey]
            elif op.has_dep:
                cnt[op.eng] += 1
                op.signal = self.eng_sem[op.eng]
                op.ticket = cnt[op.eng]
        self.n_sig = cnt

    def emit(self, eng, e):
        ops = self.ops
        waited = {}
        for op in ops:
            if op.eng != eng:
                continue
            need = {}
            for d in op.deps:
                dop = ops[d]
                if dop.signal is None:
                    continue
                if dop.eng == "pe" and eng == "pe" and dop.semkey is None:
                    continue
                key = id(dop.signal)
                if key not in need or need[key][1] < dop.ticket:
                    need[key] = (dop.signal, dop.ticket)
            for key, (sem, val) in need.items():
                if waited.get(key, 0) >= val:
                    continue
                e.wait_ge(sem, val)
                waited[key] = val
            ins = op.fn(e)
            if op.semkey is not None:
                ins.then_inc(op.signal, 16)
            elif op.signal is not None:
                ins.then_inc(op.signal, 1)
        if eng == "sp":
            need = {}
            for i in self.final_dma:
                dop = ops[i]
                key = id(dop.signal)
                if key not in need or need[key][1] < dop.ticket:
                    need[key] = (dop.signal, dop.ticket)
            for key, (sem, val) in need.items():
                e.wait_ge(sem, val)


def _interleave(a, b, lead=0):
    ia, ib = 0, 0
    while ia < min(lead, len(a)):
        a[ia]()
        ia += 1
    while ia < len(a) or ib < len(b):
        if ib < len(b):
            b[ib]()
            ib += 1
        if ia < len(a):
            a[ia]()
            ia += 1


class _Blk:
    def __init__(self):
        self.pgroups = []
        self.pre = None
        self.pieces = []
        self.fin = None


def build():
    nc = bass.Bass("TRN2", target_bir_lowering=False)

    def din(name, shape):
        return nc.dram_tensor(name, shape, F32, kind="ExternalInput").ap()

    def dout(name, shape):
        return nc.dram_tensor(name, shape, F32, kind="ExternalOutput").ap()

    xm_d = din("xm", [NTOK, D_MODEL])
    xp_d = din("xp", [NPR, D_MODEL])
    wu_d = din("wu", [D_MODEL, 32 * 512])
    wo_d = din("wo", [4096, D_MODEL])
    cw_d = din("cw", [128, 48])
    na_d = din("na", [128, 16])
    nb_d = din("nb", [128, 16])
    lbl_d = din("lbl", [128, 32])
    lng_d = din("lng", [128, D_MODEL])
    lnb_d = din("lnb", [128, D_MODEL])
    scv_d = din("scv", [32, D_MODEL])
    shg_d = din("shg", [16, 16, 128, 128])
    idf_d = din("identf", [128, 128])
    mkp_d = din("maskp", [128, 128])
    mks_d = din("masks", [128, 128])
    mk64_d = din("mk64", [128, 385])
    mkb_d = din("mkb", [128, 385])
    mr64_d = din("mr64", [128, 385])
    mrb_d = din("mrb", [128, 385])
    sel_d = din("selcol", [128, 16])
    ones_d = din("onesdiv", [128, 128])

    y_d = dout("y", [NTOK, D_MODEL])
    ncv_d = dout("ncv", [34, D_MODEL])
    nhp_d = dout("nhp", [16, 128, 128])
    nhs_d = dout("nhs", [16, 16, 128, 128])

    wu_v = wu_d.rearrange("(kc p) c -> p kc c", p=128)
    wo_v = wo_d.rearrange("(kc p) c -> p kc c", p=128)

    with ExitStack() as st:
        def T(name, shape, dt):
            return st.enter_context(nc.sbuf_tensor(name, shape, dt))

        def PS(name, shape, dt):
            return st.enter_context(nc.psum_tensor(name, shape, dt))

        BIGN = 126976
        BIG = T("big", [128, BIGN // 2], BF16)
        RREG = T("rreg", [128, 16 * NTOK], BF16)
        MIXB = T("mixb", [128, 16, NTOK], BF16)
        identb = T("identb", [128, 128], BF16)
        identf = T("identf_s", [128, 128], F32)
        maskp = T("maskp_s", [128, 128], F32)
        masks = T("masks_s", [128, 128], F32)
        mk64 = T("mk64_s", [128, 385], F32)
        mkb = T("mkb_s", [128, 385], F32)
        selcol = T("selcol_s", [128, 16], F32)
        selb = T("selb_s", [128, 16], BF16)
        mr64 = T("mr64_s", [128, 385], F32)
        mrb = T("mrb_s", [128, 385], F32)
        rowm = T("rowm_s", [128, 2], F32)
        hc0 = T("hc0_s", [128, 16], F32)
        hc1 = T("hc1_s", [128, 16], F32)
        hnc1 = T("hnc1_s", [128, 16], F32)
        onesdiv = T("onesdiv_s", [128, 128], F32)
        cw = T("cw_s", [128, 48], F32)
        na = T("na_s", [128, 16], F32)
        nb = T("nb_s", [128, 16], F32)
        lbl = T("lbl_s", [128, 32], F32)
        lb = T("lb_s", [128, 16], F32)
        oml = T("oml_s", [128, 16], F32)
        noml = T("noml_s", [128, 16], F32)
        BL = T("bl_s", [128, 24], F32)
        DL = T("dl_s", [128, 24], F32)
        dummy = T("dummy_s", [128, 4], F32)
        stat = T("stat_s", [128, 8], F32)
        SCG = T("scg_s", [32, 128], F32)

        xprevT = RREG[:, 0:16 * NPR].rearrange("p (a b) -> p a b", a=16)
        MIXA = RREG[:, :].rearrange("p (a b) -> p a b", a=16)

        pbank = {}
        for i in (0, 1, 2, 4, 5, 6, 7):
            pbank[i] = PS("pb%d" % i, [128, 512], F32)
        pT = PS("pT", [128, 1024], BF16)

        def carve(off, shape, dt):
            n = 1
            for d in shape[1:]:
                n *= d
            esz = 2 if dt == BF16 else 4
            assert off % 4 == 0
            v = BIG[:, off // 2: off // 2 + n * esz // 2]
            if dt == F32:
                v = v.bitcast(F32)
            if len(shape) == 3:
                v = v.rearrange("p (a b) -> p a b", a=shape[1])
            return v

        off = 0
        xT = carve(off, [128, 16, XC], BF16); off += 16 * XC * 2
        WSL = []
        for sl in range(2):
            WSL.append(carve(off, [128, 16, 512], BF16)); off += 16384
        STRAW = off; off += 8192
        XS = [carve(STRAW, [128, 2048], BF16), carve(STRAW + 4096, [128, 2048], BF16)]
        S0 = carve(STRAW, [128, 16, 128], F32)
        CVO = carve(STRAW, [128, 2048], F32)
        ET = [[None] * 3 for _ in range(2)]
        for par in range(2):
            for i in range(3):
                ET[par][i] = carve(off, [128, TW], F32); off += TW * 4
        G = []
        for i in range(7):
            G.append(carve(off, [128, TW], F32)); off += TW * 4
        QE, KE, KTC, DLT = [], [], [], []
        for par in range(2):
            QE.append(carve(off, [128, TW], BF16)); off += TW * 2
            KE.append(carve(off, [128, TW], BF16)); off += TW * 2
            KTC.append(carve(off, [128, TW], BF16)); off += TW * 2
            DLT.append(carve(off, [128, 24], F32)); off += 96
        VCM = carve(off, [128, TW], BF16); off += TW * 2
        VT = []
        for par in range(2):
            VT.append(carve(off, [128, 384], BF16)); off += 768
        KT = []
        SCM = []
        for i in range(3):
            KT.append(carve(off, [128, 128], BF16)); off += 256
        for i in range(3):
            SCM.append(carve(off, [128, 128], BF16)); off += 256
        SST = [[None, None], [None, None]]
        for i in range(2):
            for r in range(2):
                SST[i][r] = carve(off, [128, 128], F32); off += 512
        NSBF = 6
        SBF = []
        for i in range(NSBF):
            SBF.append(carve(off, [128, 128], BF16)); off += 256
        HG_OVL = off
        S0bf = carve(off, [128, 16, 128], BF16); off += 4096
        VBLK = carve(off, [128, 16, 128], BF16); off += 4096
        UB = carve(HG_OVL, [128, XC], F32)
        USM = carve(HG_OVL + XC * 4, [128, 16, 10], F32)
        UO = carve(HG_OVL + XC * 4 + 640, [128, 16, 34], F32)
        off = max(off, HG_OVL + XC * 4 + 640 + 16 * 34 * 4)
        V2 = []
        IDL = []
        for par in range(2):
            V2.append(carve(off, [128, 3 * 256], BF16)); off += 1536
            IDL.append(carve(off, [128, 24], F32)); off += 96
        assert off <= BIGN, off
        STK = [("XS", 0), ("XS", 1)]

        HB = carve(0, [128, 9, D_MODEL], F32)
        WO = [carve(73728, [128, 32, 256], BF16), carve(73728 + 16384, [128, 32, 256], BF16)]
        LNG = carve(106496, [128, D_MODEL], F32)
        LNB = carve(106496 + 8192, [128, D_MODEL], F32)

        s = Sched(nc)

        def dma(q, out, in_, reads=(), writes=(), semkey=None, final=False):
            return s.add(q, lambda e: e.dma_start(out=out, in_=in_), reads=reads, writes=writes,
                         semkey=semkey, final=final)

        def bk(i):
            return ("bank", i)

        for dst, src, key in [(identf, idf_d, "identf"), (maskp, mkp_d, "maskp"), (masks, mks_d, "masks"),
                              (mk64, mk64_d, "mk64"), (mkb, mkb_d, "mkb"), (selcol, sel_d, "selcol"),
                              (mr64, mr64_d, "mr64"), (mrb, mrb_d, "mrb"),
                              (onesdiv, ones_d, "onesdiv"), (cw, cw_d, "cw"), (na, na_d, "na"),
                              (nb, nb_d, "nb"), (lbl, lbl_d, "lbl")]:
            dma("sp", dst[:], src[:, :], writes=[key], semkey="c_" + key)
        dma("pool", identb[:], idf_d[:, :], writes=["identb"], semkey="c_identb")
        dma("pool", selb[:], sel_d[:, :], writes=["selb"], semkey="c_selb")
        s.add("dve", lambda e: e.reduce_sum(out=rowm[:, 0:1], in_=selcol[:, 0:8], axis=mybir.AxisListType.X),
              reads=["selcol"], writes=["rowm"])
        s.add("dve", lambda e: e.reduce_sum(out=rowm[:, 1:2], in_=selcol[:, 8:16], axis=mybir.AxisListType.X),
              reads=["selcol"], writes=["rowm"])
        s.add("dve", lambda e: e.memset(G[0][:], 0.5), writes=["G0"])
        s.add("dve", lambda e: e.memset(dummy[:], 0.0), writes=["dummy"])

        s.add("act", lambda e: e.activation(out=lbl[:], in_=lbl[:], func=AF.Exp), reads=["lbl"], writes=["lbl"])
        s.add("dve", lambda e: e.tensor_tensor(out=oml[:], in0=lbl[:, 0:16], in1=lbl[:, 16:32], op=ALU.add),
              reads=["lbl"], writes=["oml"])
        s.add("dve", lambda e: e.reciprocal(out=oml[:], in_=oml[:]), reads=["oml"], writes=["oml"])
        s.add("dve", lambda e: e.tensor_tensor(out=lb[:], in0=lbl[:, 0:16], in1=oml[:], op=ALU.mult),
              reads=["lbl", "oml"], writes=["lb"])
        s.add("dve", lambda e: e.tensor_scalar(out=oml[:], in0=lb[:], scalar1=-1.0, scalar2=1.0,
                                               op0=ALU.mult, op1=ALU.add), reads=["lb"], writes=["oml"])
        s.add("dve", lambda e: e.tensor_scalar(out=noml[:], in0=oml[:], scalar1=-1.0, scalar2=None,
                                               op0=ALU.mult), reads=["oml"], writes=["noml"])
        s.add("dve", lambda e: e.tensor_scalar(out=hc1[:], in0=oml[:], scalar1=0.5, scalar2=None, op0=ALU.mult),
              reads=["oml"], writes=["hc1"])
        s.add("dve", lambda e: e.tensor_scalar(out=hnc1[:], in0=oml[:], scalar1=-0.5, scalar2=None, op0=ALU.mult),
              reads=["oml"], writes=["hnc1"])
        s.add("dve", lambda e: e.tensor_tensor(out=hc0[:], in0=lb[:], in1=hc1[:], op=ALU.add),
              reads=["lb", "hc1"], writes=["hc0"])

        def load_unit_weights(u):
            sl = u % 2
            for pc in range(4):
                dma("pool", WSL[sl][:, pc * 4:(pc + 1) * 4, :], wu_v[:, pc * 4:(pc + 1) * 4, u * 512:(u + 1) * 512],
                    writes=[("w", sl, pc)], semkey="w%d_%d" % (sl, pc))

        def wkeys(sl):
            return [("w", sl, pc) for pc in range(4)]

        load_unit_weights(0)

        tcount = [0]

        XTK = [("xT", t) for t in range(9)] + ["xTh"]
        XPK = [("xp", t) for t in range(8)]

        XS4 = [XS[0], XS[1], S0bf[:, :, :].rearrange("p a b -> p (a b)"), VBLK[:, :, :].rearrange("p a b -> p (a b)")]
        XSK = [("XS", 0), ("XS", 1), "S0bf", "VBLK"]

        def load_xT(src_rows, dstT, col0, wkey):
            t = tcount[0]
            tcount[0] += 1
            par = t % 4
            xs_ap = XS4[par]
            xs_key = XSK[par]
            dma("pool", xs_ap[:], src_rows, writes=[xs_key], semkey="xs%d" % par)
            for half in range(2):
                if (2 * t + half) % 2 == 0:
                    bank_ap = pT[:, :]
                    bkey = "pT"
                else:
                    bank_ap = pbank[0][:].bitcast(BF16)
                    bkey = bk(0)
                for k8 in range(8):
                    kc = half * 8 + k8
                    s.add("pe", lambda e, kc=kc, k8=k8, bank_ap=bank_ap, xs_ap=xs_ap: e.transpose(
                        bank_ap[:, k8 * 128:(k8 + 1) * 128], xs_ap[:, kc * 128:(kc + 1) * 128], identb[:]),
                        reads=[xs_key, "identb"], writes=[bkey])
                eng = "act" if half == 0 else "dve"
                dst = dstT[:, half * 8:(half + 1) * 8, col0:col0 + 128]
                src = bank_ap.rearrange("p (a b) -> p a b", a=8)
                if eng == "act":
                    s.add("act", lambda e, dst=dst, src=src: e.copy(out=dst, in_=src), reads=[bkey], writes=[wkey])
                else:
                    s.add("dve", lambda e, dst=dst, src=src: e.tensor_copy(out=dst, in_=src), reads=[bkey], writes=[wkey])

        for t in range(8):
            load_xT(xp_d[t * 128:(t + 1) * 128, :], xprevT, t * 128, ("xp", t))
        s.add("dve", lambda e: e.tensor_copy(out=xT[:, :, 0:2], in_=xprevT[:, :, NPR - 2:NPR]),
              reads=XPK, writes=["xTh"])

        proj_rot = {"banks": [0, 1], "i": 0}

        def next_bank():
            b = proj_rot["banks"][proj_rot["i"] % len(proj_rot["banks"])]
            proj_rot["i"] += 1
            return b

        def proj_cm(u, j, src, c0, n, bank):
            sl = u % 2
            for kc in range(16):
                s.add("pe", lambda e, kc=kc: e.matmul(pbank[bank][:, 0:n], WSL[sl][:, kc, j * 128:(j + 1) * 128],
                                                       src[:, kc, c0:c0 + n], start=(kc == 0), stop=(kc == 15)),
                      reads=[("w", sl, kc // 4)] + (XTK if src is xT else XPK), writes=[bk(bank)])

        def proj_tm(u, j, src, c0, ntile):
            sl = u % 2
            for tau in range(ntile):
                for kc in range(16):
                    s.add("pe", lambda e, kc=kc, tau=tau: e.matmul(
                        pbank[4][:, tau * 128:(tau + 1) * 128], src[:, kc, c0 + tau * 128:c0 + (tau + 1) * 128],
                        WSL[sl][:, kc, j * 128:(j + 1) * 128], start=(kc == 0), stop=(kc == 15)),
                        reads=[("w", sl, kc // 4)] + (XTK if src is xT else XPK), writes=[bk(4)])

        DSB = [pbank[7][:, 0:128], pbank[4][:, 384:512]]
        DSK = [bk(7), bk(4)]
        blk_count = [0]
        tile_ctr = [0]
        ds_ctr = [0]
        snap_ctr = [0]
        hstates = [{"sidx": 0} for _ in range(17)]

        def make_hgrn_block(h, main, bi):
            u = h
            blk = _Blk()
            par = blk_count[0] % 2
            blk_count[0] += 1
            hp = h % 2
            if main:
                src = xT
                c0 = 2 + 384 * bi
                n = 384
                segs = [(0, 6, 64)] if bi < 2 else [(0, 4, 64), (256, 16, 8)]
                tiles = ["P", "P", "P"] if bi < 2 else ["P", "P", "S"]
                m0, m1 = (mk64, mr64) if bi < 2 else (mkb, mrb)
                tok0 = 384 * bi
            else:
                src = xprevT
                c0 = 384 * bi
                n = 384 if bi < 2 else 256
                segs = [(0, n // 64, 64)]
                tiles = ["P"] * (n // 128)
                m0, m1 = mk64, mr64
                tok0 = None
            ntile = len(tiles)
            Eth, Eq, Esz = ET[par]
            V = VT[par]
            QEp, KEp, KTCp, DLp = QE[par], KE[par], KTC[par], DLT[par]
            hs = slice(h, h + 1)
            dl_idx = {}
            chain_state = {}
            kq = ("QE", par)
            kk_ = ("KE", par)
            kkt = ("KTC", par)
            kdl = ("DL", par)
            kidl = ("IDL", par)
            IDLp = IDL[par]
            V2p = V2[par]
            hst = hstates[h]

            def g_f():
                b = next_bank()
                proj_cm(u, 1, src, c0, n, b)
                s.add("act", lambda e: e.activation(out=Eth[:, 0:n], in_=pbank[b][:, 0:n], func=AF.Tanh, scale=0.5),
                      reads=[bk(b)], writes=[("E", par, 0)])
                s.add("act", lambda e: e.activation(out=G[0][:, 0:n], in_=Eth[:, 0:n], func=AF.Identity,
                                                    scale=hc1[:, hs], bias=hc0[:, hs]),
                      reads=[("E", par, 0), "hc0", "hc1"], writes=["G0"])
                s.add("act", lambda e: e.activation(out=G[1][:, 0:n], in_=Eth[:, 0:n], func=AF.Identity,
                                                    scale=hnc1[:, hs], bias=hc1[:, hs]),
                      reads=[("E", par, 0), "hnc1", "hc1"], writes=["G1"])
                def rev(ap2d):
                    (ps_, pn_), (fs_, fn_) = ap2d.ap
                    return bass.AP(ap2d.tensor, ap2d.offset + (fn_ - 1) * fs_, [[ps_, pn_], [-fs_, fn_]])

                s.add("dve", lambda e: e.tensor_tensor(out=G[2][:, 0:n], in0=G[0][:, 1:n + 1], in1=m0[:, 1:n + 1],
                                                       op=ALU.mult), reads=["G0", "mk64", "mkb"], writes=["G2"])
                s.add("dve", lambda e: e.tensor_tensor_scan(out=rev(G[3][:, 0:n]), data0=rev(G[2][:, 0:n]),
                                                            data1=rev(m1[:, 1:n + 1]), initial=0.0,
                                                            op0=ALU.mult, op1=ALU.add),
                      reads=["G2", "mr64", "mrb"], writes=["G3"])
                if main:
                    s.add("dve", lambda e: e.tensor_tensor(out=G[2][:, 0:n], in0=G[1][:, 0:n], in1=G[3][:, 0:n],
                                                           op=ALU.mult), reads=["G1", "G3"], writes=["G2"])
                else:
                    s.add("dve", lambda e: e.tensor_tensor(out=KTCp[:, 0:n], in0=G[1][:, 0:n], in1=G[3][:, 0:n],
                                                           op=ALU.mult), reads=["G1", "G3"], writes=[kkt])
                o = 0
                for (sc0, nch, L) in segs:
                    fv = G[0][:, sc0:sc0 + nch * L].rearrange("p (c t) -> p c t", t=L)[:, :, 0]
                    rv = G[3][:, sc0:sc0 + nch * L].rearrange("p (c t) -> p c t", t=L)[:, :, 0]
                    s.add("dve", lambda e, fv=fv, rv=rv, o=o, nch=nch: e.tensor_tensor(
                        out=DLp[:, o:o + nch], in0=fv, in1=rv, op=ALU.mult), reads=["G0", "G3"], writes=[kdl])
                    for ci in range(nch):
                        dl_idx[(sc0, ci)] = o + ci
                    o += nch
                chain_state['ntot'] = o

            def g_q():
                b = next_bank()
                proj_cm(u, 0, src, c0, n, b)
                s.add("act", lambda e: e.activation(out=Eq[:, 0:n], in_=pbank[b][:, 0:n], func=AF.Silu),
                      reads=[bk(b)], writes=[("E", par, 1)])
                ntot = chain_state['ntot']
                if main:
                    s.add("act", lambda e: e.copy(out=KTCp[:, 0:n], in_=G[2][:, 0:n]), reads=["G2"], writes=[kkt])
                    s.add("dve", lambda e: e.reciprocal(out=IDLp[:, 0:ntot], in_=DLp[:, 0:ntot]),
                          reads=[kdl], writes=[kidl])
                    o = 0
                    for (sc0, nch, L) in segs:
                        src3 = G[2][:, sc0:sc0 + nch * L].rearrange("p (c t) -> p c t", t=L)
                        dst3 = KEp[:, sc0:sc0 + nch * L].rearrange("p (c t) -> p c t", t=L)
                        dlb = IDLp[:, o:o + nch].unsqueeze(2).to_broadcast([128, nch, L])
                        s.add("dve", lambda e, src3=src3, dst3=dst3, dlb=dlb: e.tensor_tensor(
                            out=dst3, in0=src3, in1=dlb, op=ALU.mult), reads=["G2", kidl], writes=[kk_])
                        o += nch

                if main:
                    s.add("dve", lambda e: e.tensor_tensor(out=G[3][:, 0:n], in0=G[0][:, 0:n], in1=m0[:, 0:n],
                                                           op=ALU.mult), reads=["G0", "mk64", "mkb"], writes=["G3"])
                    s.add("dve", lambda e: e.tensor_tensor(out=G[2][:, 0:n], in0=G[0][:, 0:n], in1=m1[:, 0:n],
                                                           op=ALU.mult), reads=["G0", "mr64", "mrb"], writes=["G2"])
                    s.add("dve", lambda e: e.tensor_tensor_scan(out=G[4][:, 0:n], data0=G[3][:, 0:n],
                                                                data1=G[2][:, 0:n], initial=0.0,
                                                                op0=ALU.mult, op1=ALU.add),
                          reads=["G2", "G3"], writes=["G4"])
                s.add("dve", lambda e: e.scalar_tensor_tensor(out=QEp[:, 0:n], in0=Eq[:, 0:n], scalar=QSCALE,
                                                              in1=G[4][:, 0:n], op0=ALU.mult, op1=ALU.mult),
                      reads=[("E", par, 1), "G4"], writes=[kq])

            def g_v():
                b = next_bank()
                proj_cm(u, 2, src, c0, n, b)
                s.add("act", lambda e: e.copy(out=VCM[:, 0:n], in_=pbank[b][:, 0:n]), reads=[bk(b)], writes=["VCM"])

            def g_vt():
                aux16 = pbank[4][:].bitcast(BF16)
                for tau in range(ntile):
                    s.add("pe", lambda e, tau=tau: e.transpose(aux16[:, tau * 128:(tau + 1) * 128],
                                                               VCM[:, tau * 128:(tau + 1) * 128], identb[:]),
                          reads=["VCM", "identb"], writes=[bk(4)])
                s.add("act", lambda e: e.copy(out=V[:, 0:n], in_=aux16[:, 0:n]), reads=[bk(4)], writes=[("V", par)])
                if main and bi == 2:
                    s.add("dve", lambda e: e.tensor_tensor(
                        out=VBLK[:, :, :], in0=V[:, 256:384].unsqueeze(1).to_broadcast([128, 16, 128]),
                        in1=selb[:, :].unsqueeze(2).to_broadcast([128, 16, 128]), op=ALU.mult),
                        reads=[("V", par), "selb"], writes=["VBLK"])
                for c in range(2):
                    s.add("act", lambda e, c=c: e.activation(
                        out=V2p[:, 0:ntile * 256].rearrange("p (t c v) -> p t c v", c=2, v=128)[:, :, c, :],
                        in_=aux16[:, 0:n].rearrange("p (t v) -> p t v", v=128), func=AF.Identity,
                        scale=rowm[:, c:c + 1]), reads=[bk(4), "rowm"], writes=[("V2", par)])

            def g_z():
                g_vt()
                b = next_bank()
                proj_cm(u, 3, src, c0, n, b)
                s.add("act", lambda e: e.activation(out=Esz[:, 0:n], in_=pbank[b][:, 0:n], func=AF.Silu),
                      reads=[bk(b)], writes=[("E", par, 2)])

            blk.pgroups = [g_f, g_q, g_v, g_z] if main else [g_f, g_v, g_vt]

            snaps = {}
            scm_slot = {}
            kt_slot = {}

            def make_T(tau):
                def T_():
                    tc = slice(tau * 128, (tau + 1) * 128)
                    kt = tile_ctr[0] % 3
                    tile_ctr[0] += 1
                    kt_slot[tau] = kt
                    scm_slot[tau] = kt
                    s.add("pe", lambda e: e.transpose(pT[:, 0:128], KTCp[:, tc], identb[:]),
                          reads=[kkt, "identb"], writes=["pT"])
                    s.add("act", lambda e: e.copy(out=KT[kt][:], in_=pT[:, 0:128]), reads=["pT"], writes=[("KT", kt)])
                    if main:
                        s.add("pe", lambda e: e.matmul(pbank[5][:, 0:128], KEp[:, tc], QEp[:, tc], start=True, stop=True),
                              reads=[kk_, kq], writes=[bk(5)])
                        mk = maskp if tiles[tau] == "P" else masks
                        s.add("dve", lambda e: e.tensor_tensor(out=SCM[kt][:], in0=pbank[5][:, 0:128], in1=mk[:],
                                                               op=ALU.mult),
                              reads=[bk(5), "maskp", "masks"], writes=[("SCM", kt)])
                return T_

            def make_D(tau):
                def D_():
                    tc = slice(tau * 128, (tau + 1) * 128)
                    kt = kt_slot[tau]
                    if tiles[tau] == "P":
                        dsb = 7 if ds_ctr[0] % 2 == 0 else 2
                        ds_ctr[0] += 1
                        s.add("pe", lambda e: e.matmul(pbank[dsb][:, 0:256], KT[kt][:],
                                                       V2p[:, tau * 256:(tau + 1) * 256], start=True, stop=True),
                              reads=[("KT", kt), ("V2", par)], writes=[bk(dsb)])
                        for ci in range(2):
                            cur = hst["sidx"] % 2
                            nxt = (hst["sidx"] + 1) % 2
                            hst["sidx"] += 1
                            if main:
                                sb = snap_ctr[0] % NSBF
                                snap_ctr[0] += 1
                                snaps[(tau, ci)] = sb
                                s.add("dve", lambda e, sb=sb, cur=cur: e.tensor_copy(out=SBF[sb][:],
                                                                                      in_=SST[hp][cur][:]),
                                      reads=[("SST", hp, cur)], writes=[("SBF", sb)])
                            di = dl_idx[(0, tau * 2 + ci)]
                            s.add("dve", lambda e, di=di, ci=ci, cur=cur, nxt=nxt: e.scalar_tensor_tensor(
                                out=SST[hp][nxt][:], in0=SST[hp][cur][:], scalar=DLp[:, di:di + 1],
                                in1=pbank[dsb][:, ci * 128:(ci + 1) * 128], op0=ALU.mult, op1=ALU.add),
                                reads=[("SST", hp, cur), kdl, bk(dsb)], writes=[("SST", hp, nxt)])
                    else:
                        for q in range(4):
                            db = 7 if q % 2 == 0 else 5
                            s.add("pe", lambda e, q=q, db=db: e.matmul(
                                pbank[db][:, 0:512], KT[kt][:],
                                VBLK[:, :, :].rearrange("p a b -> p (a b)")[:, 512 * q:512 * (q + 1)],
                                start=True, stop=True),
                                reads=[("KT", kt), "VBLK"], writes=[bk(db)])
                            di = dl_idx[(256, 4 * q)]
                            s0q = S0[:, 4 * q:4 * q + 4, :]
                            s.add("dve", lambda e, s0q=s0q, di=di: e.tensor_tensor(
                                out=s0q, in0=s0q, in1=DLp[:, di:di + 4].unsqueeze(2).to_broadcast([128, 4, 128]),
                                op=ALU.mult), reads=STK + [kdl], writes=STK)
                            s.add("dve", lambda e, s0q=s0q, db=db: e.tensor_tensor(
                                out=s0q, in0=s0q, in1=pbank[db][:, 0:512].rearrange("p (j v) -> p j v", j=4),
                                op=ALU.add), reads=STK + [bk(db)], writes=STK)
                        dma("sp", nhs_d[:, h, :, :].rearrange("j k v -> k j v"), S0[:, :, :], reads=STK,
                            semkey="nhs", final=True)
                return D_

            def make_B(tau):
                def B_():
                    tc = slice(tau * 128, (tau + 1) * 128)
                    sc = scm_slot[tau]
                    s.add("pe", lambda e: e.matmul(pbank[6][:, 0:128], V[:, tc], SCM[sc][:], start=True, stop=False),
                          reads=[("V", par), ("SCM", sc)], writes=[bk(6)])
                    if tiles[tau] == "P":
                        for ci in range(2):
                            sb = snaps[(tau, ci)]
                            qc = slice(tau * 128 + ci * 64, tau * 128 + (ci + 1) * 64)
                            s.add("pe", lambda e, qc=qc, sb=sb, ci=ci: e.matmul(
                                pbank[6][:, ci * 64:(ci + 1) * 64], SBF[sb][:], QEp[:, qc],
                                start=False, stop=(ci == 1)),
                                reads=[("SBF", sb), kq], writes=[bk(6)])
                    else:
                        for j in range(16):
                            qc = slice(tau * 128 + 8 * j, tau * 128 + 8 * j + 8)
                            s.add("pe", lambda e, j=j, qc=qc: e.matmul(
                                pbank[6][:, 8 * j:8 * j + 8], S0bf[:, j, :], QEp[:, qc], start=False, stop=(j == 15)),
                                reads=["S0bf", kq], writes=[bk(6)])
                    s.add("act", lambda e: e.copy(out=G[5][:, tc], in_=pbank[6][:, 0:128]),
                          reads=[bk(6)], writes=["G5"])
                return B_

            nst = ntile + (2 if main else 1)
            pcs = []
            for i in range(nst):
                parts = []
                if i < ntile:
                    parts.append(make_T(i))
                if 0 <= i - 1 < ntile:
                    parts.append(make_D(i - 1))
                if main and 0 <= i - 2 < ntile:
                    parts.append(make_B(i - 2))
                pcs.append(lambda parts=parts: [p() for p in parts])
            blk.pieces = pcs

            def fin():
                s.add("act", lambda e: e.activation(out=G[6][:, 0:n], in_=G[5][:, 0:n], func=AF.Square),
                      reads=["G5"], writes=["G6"])
                s.add("pe", lambda e: e.matmul(pbank[4][:, 0:n], onesdiv[:], G[6][:, 0:n], start=True, stop=True),
                      reads=["onesdiv", "G6"], writes=[bk(4)])
                s.add("act", lambda e: e.activation(out=G[6][:, 0:n], in_=pbank[4][:, 0:n], func=AF.Ln, bias=EPS),
                      reads=[bk(4)], writes=["G6"])
                s.add("act", lambda e: e.activation(out=G[6][:, 0:n], in_=G[6][:, 0:n], func=AF.Exp, scale=-0.5),
                      reads=["G6"], writes=["G6"])
                s.add("dve", lambda e: e.tensor_tensor(out=G[5][:, 0:n], in0=G[5][:, 0:n], in1=G[6][:, 0:n], op=ALU.mult),
                      reads=["G5", "G6"], writes=["G5"])
                s.add("dve", lambda e: e.scalar_tensor_tensor(out=MIXB[:, h, tok0:tok0 + n], in0=G[5][:, 0:n],
                                                              scalar=nb[:, hs], in1=Esz[:, 0:n],
                                                              op0=ALU.mult, op1=ALU.mult),
                      reads=["G5", "nb", ("E", par, 2)], writes=[("mixb", h)])
                if bi == 2:
                    cur = hst["sidx"] % 2
                    dma("sp", nhp_d[h, :, :], SST[hp][cur][:], reads=[("SST", hp, cur)], semkey="nhp%d" % hp,
                        final=True)

            blk.fin = fin if main else None
            return blk

        def make_conv_block(g, bi):
            u = 16 + g
            blk = _Blk()
            par = blk_count[0] % 2
            blk_count[0] += 1
            c0 = 0 if bi == 0 else 2 + 384 * bi
            n = 386 if bi == 0 else 384
            t_off = 2 if bi == 0 else 0
            tok0 = 384 * bi
            a0 = 2 + tok0
            npr = 384 if bi < 2 else 256
            Ev, Eb, Esz = ET[par]

            def g_v():
                b = next_bank()
                proj_cm(u, 0, xT, c0, n, b)
                s.add("act", lambda e: e.copy(out=Ev[:, 0:n], in_=pbank[b][:, 0:n]), reads=[bk(b)],
                      writes=[("E", par, 0)])

            def g_c():
                b = next_bank()
                proj_cm(u, 2, xT, c0, n, b)
                s.add("dve", lambda e: e.tensor_tensor(out=UB[:, c0:c0 + n], in0=pbank[b][:, 0:n], in1=Ev[:, 0:n],
                                                       op=ALU.mult),
                      reads=[bk(b), ("E", par, 0)], writes=["UB"])

            def g_b():
                b = next_bank()
                proj_cm(u, 1, xT, c0, n, b)
                s.add("act", lambda e: e.copy(out=Eb[:, 0:n], in_=pbank[b][:, 0:n]), reads=[bk(b)],
                      writes=[("E", par, 1)])

            def g_z():
                b = next_bank()
                proj_cm(u, 3, xT, c0, n, b)
                s.add("act", lambda e: e.activation(out=Esz[:, 0:n], in_=pbank[b][:, 0:n], func=AF.Silu),
                      reads=[bk(b)], writes=[("E", par, 2)])

            blk.pgroups = [g_v, g_c, g_b, g_z]

            def w(jj):
                return cw[:, g * 3 + jj:g * 3 + jj + 1]

            def c1():
                YC = G[0]
                s.add("dve", lambda e: e.tensor_scalar(out=YC[:, 0:npr], in0=UB[:, a0:a0 + npr], scalar1=w(2),
                                                       scalar2=None, op0=ALU.mult),
                      reads=["UB", "cw"], writes=["G0"])
                for jj, sh in ((1, 1), (0, 2)):
                    s.add("dve", lambda e, jj=jj, sh=sh: e.scalar_tensor_tensor(
                        out=YC[:, 0:npr], in0=UB[:, a0 - sh:a0 - sh + npr], scalar=w(jj), in1=YC[:, 0:npr],
                        op0=ALU.mult, op1=ALU.add), reads=["UB", "cw", "G0"], writes=["G0"])
                if bi == 2:
                    dma("sp", SCG[:], scv_d[:, g * 128:(g + 1) * 128], writes=["SCG"], semkey="scg")
                    s.add("pe", lambda e: e.transpose(pbank[4][:, 0:32], SCG[:], identf[0:32, 0:32]),
                          reads=["SCG", "identf"], writes=[bk(4)])
                    s.add("dve", lambda e: e.tensor_copy(
                        out=USM[:, :, 0:2], in_=pbank[4][:, 0:32].rearrange("p (j r) -> p j r", r=2)),
                        reads=[bk(4)], writes=["USM"])
                    s.add("dve", lambda e: e.tensor_copy(
                        out=USM[:, :, 2:10], in_=UB[:, 2 + NPR:2 + NTOK].rearrange("p (j t) -> p j t", t=8)),
                        reads=["UB"], writes=["USM"])
                    YS = YC[:, 256:384].rearrange("p (j t) -> p j t", t=8)
                    s.add("dve", lambda e: e.tensor_scalar(out=YS, in0=USM[:, :, 2:10], scalar1=w(2), scalar2=None,
                                                           op0=ALU.mult), reads=["USM", "cw"], writes=["G0"])
                    for jj, sh in ((1, 1), (0, 2)):
                        s.add("dve", lambda e, jj=jj, sh=sh: e.scalar_tensor_tensor(
                            out=YS, in0=USM[:, :, 2 - sh:10 - sh], scalar=w(jj), in1=YS,
                            op0=ALU.mult, op1=ALU.add), reads=["USM", "cw", "G0"], writes=["G0"])
                    s.add("dve", lambda e: e.tensor_copy(out=UO[:, g, 0:2], in_=UB[:, NPR:NPR + 2]),
                          reads=["UB"], writes=["UO"])
                    s.add("dve", lambda e: e.tensor_copy(
                        out=UO[:, g, 2:34].rearrange("p (j r) -> p j r", r=2), in_=USM[:, :, 8:10]),
                        reads=["USM"], writes=["UO"])
                s.add("dve", lambda e: e.tensor_tensor(out=G[1][:, 0:384], in0=Eb[:, t_off:t_off + 384],
                                                       in1=YC[:, 0:384], op=ALU.mult),
                      reads=[("E", par, 1), "G0"], writes=["G1"])
                s.add("act", lambda e: e.activation(out=G[2][:, 0:384], in_=G[1][:, 0:384], func=AF.Square),
                      reads=["G1"], writes=["G2"])

            def c2():
                gs = slice(g, g + 1)
                s.add("pe", lambda e: e.matmul(pbank[4][:, 0:384], onesdiv[:], G[2][:, 0:384], start=True, stop=True),
                      reads=["onesdiv", "G2"], writes=[bk(4)])
                s.add("act", lambda e: e.activation(out=G[2][:, 0:384], in_=pbank[4][:, 0:384], func=AF.Ln, bias=EPS),
                      reads=[bk(4)], writes=["G2"])
                s.add("act", lambda e: e.activation(out=G[2][:, 0:384], in_=G[2][:, 0:384], func=AF.Exp, scale=-0.5),
                      reads=["G2"], writes=["G2"])
                s.add("dve", lambda e: e.tensor_tensor(out=G[1][:, 0:384], in0=G[1][:, 0:384], in1=G[2][:, 0:384],
                                                       op=ALU.mult), reads=["G1", "G2"], writes=["G1"])
                s.add("dve", lambda e: e.scalar_tensor_tensor(out=MIXA[:, g, tok0:tok0 + 384], in0=G[1][:, 0:384],
                                                              scalar=na[:, gs], in1=Esz[:, t_off:t_off + 384],
                                                              op0=ALU.mult, op1=ALU.mult),
                      reads=["G1", "na", ("E", par, 2)], writes=[("mixa", g)])

            blk.pieces = [c1, c2]
            return blk

        LEAD = 0

        def run_block(blk, prev, init=None):
            _interleave(blk.pgroups, prev.pieces if prev is not None else [],
                        lead=(1 if len(blk.pgroups) >= 4 else 0))
            if prev is not None and prev.fin is not None:
                prev.fin()
            if init is not None:
                init()
            return blk

        def run_blocks(blocks, prev):
            for blk in blocks:
                prev = run_block(blk, prev)
            return prev

        def init_state(h):
            hp = h % 2
            hstates[h]["sidx"] = 0
            s.add("dve", lambda e: e.memset(SST[hp][0][:], 0.0), writes=[("SST", hp, 0)])

        def init_sample(h):
            dma("sp", S0[:, :, :], shg_d[:, h, :, :].rearrange("j k v -> k j v"), writes=STK, semkey="st")
            dma("pool", S0bf[:, :, :], shg_d[:, h, :, :].rearrange("j k v -> k j v"), writes=["S0bf"], semkey="stb")

        load_unit_weights(1)
        prev = None
        init_state(0)
        for bi in range(3):
            prev = run_block(make_hgrn_block(0, False, bi), prev, (lambda: init_sample(0)) if bi == 2 else None)
            if bi < 2:
                for t in (range(0, 5) if bi == 0 else range(5, 9)):
                    load_xT(xm_d[t * 128:(t + 1) * 128, :], xT, 2 + t * 128, ("xT", t))
        for h in range(16):
            nh = h + 1
            prev = run_block(make_hgrn_block(h, True, 0), prev)
            prev = run_block(make_hgrn_block(h, True, 1), prev)
            if nh < 16:
                init_state(nh)
                prev = run_block(make_hgrn_block(nh, False, 0), prev)
            prev = run_block(make_hgrn_block(h, True, 2), prev)
            load_unit_weights(h + 2)
            if nh < 16:
                prev = run_block(make_hgrn_block(nh, False, 1), prev, lambda nh=nh: init_sample(nh))
                prev = run_block(make_hgrn_block(nh, False, 2), prev)
        _interleave([], prev.pieces)
        if prev.fin is not None:
            prev.fin()
        prev = None
        s.add("dve", lambda e: e.memset(dummy[:, 0:1], 0.0),
              writes=["dummy", "S0bf", "VBLK", "UB", "USM", "UO"] + XPK + [("mixa", g) for g in range(16)])
        proj_rot["banks"] = [0, 1, 2, 5, 6, 7]
        for g in range(16):
            u = 16 + g
            if g >= 1 and u + 1 < 32:
                load_unit_weights(u + 1)
            blocks = [make_conv_block(g, bi) for bi in range(3)]
            prev = run_blocks(blocks, prev)
        _interleave([], prev.pieces)

        for g in range(16):
            b = 5 + (g // 4) % 2
            s.add("pe", lambda e, g=g, b=b: e.transpose(pbank[b][0:34, (g % 4) * 128:(g % 4 + 1) * 128],
                                                        UO[:, g, :], identf[:]),
                  reads=["UO", "identf"], writes=[bk(b)])
            if g % 4 == 3:
                gq = g // 4
                s.add("dve", lambda e, gq=gq, b=b: e.tensor_copy(out=CVO[0:34, gq * 512:(gq + 1) * 512],
                                                                 in_=pbank[b][0:34, :]),
                      reads=[bk(b)], writes=STK)
        dma("sp", ncv_d[:, :], CVO[0:34, :], reads=STK, semkey="ncv", final=True)

        s.barrier("dve", lambda e: e.memset(dummy[:, 3:4], 0.0))
        for t in range(9):
            dma("sp", HB[:, t, :], xm_d[t * 128:(t + 1) * 128, :], writes=[("hb", t)], semkey="hb%d" % t)
        dma("sp", LNG[:], lng_d[:, :], writes=["lng"], semkey="c_lng")
        dma("sp", LNB[:], lnb_d[:, :], writes=["lnb"], semkey="c_lnb")

        def load_wo(n8):
            sl = n8 % 2
            for pc in range(4):
                dma("pool", WO[sl][:, pc * 8:(pc + 1) * 8, :], wo_v[:, pc * 8:(pc + 1) * 8, n8 * 256:(n8 + 1) * 256],
                    writes=[("wo", sl, pc)], semkey="wo%d_%d" % (sl, pc))

        stat = T("stat2_s", [128, 16], F32)
        rsum = T("rsum_s", [128, 72], F32)
        s.add("dve", lambda e: e.memset(rsum[:], 0.0), writes=["rsum"])
        JUNK = carve(73728, [128, D_MODEL], F32)
        jkeys = [("wo", 0, pc) for pc in range(4)]

        def ln_stats(t):
            hv = HB[:, t, :]
            p8 = (t % 2) * 8
            c_sum, c_nm, c_ssq, c_rstd = (stat[:, p8 + i:p8 + i + 1] for i in range(4))
            sk = ("stat", t % 2)
            s.add("dve", lambda e: e.memset(stat[:, p8:p8 + 8], 0.0), writes=[sk])
            s.add("dve", lambda e: e.reduce_sum(out=c_sum, in_=rsum[:, t * 8:(t + 1) * 8], axis=mybir.AxisListType.X),
                  reads=[("hb", t), "rsum", sk], writes=[sk])
            s.add("dve", lambda e: e.tensor_scalar(out=c_nm, in0=c_sum, scalar1=-1.0 / D_MODEL, scalar2=None,
                                                   op0=ALU.mult), reads=[sk], writes=[sk])
            s.add("act", lambda e: e.activation(out=hv, in_=hv, func=AF.Identity, bias=c_nm),
                  reads=[("hb", t), sk], writes=[("hb", t)])
            s.add("act", lambda e: e.activation(out=JUNK[:, :], in_=hv, func=AF.Square, accum_out=c_ssq),
                  reads=[("hb", t), sk], writes=[sk] + jkeys)
            s.add("act", lambda e: e.activation(out=c_rstd, in_=c_ssq, func=AF.Ln, scale=1.0 / D_MODEL, bias=EPS),
                  reads=[sk], writes=[sk])
            s.add("act", lambda e: e.activation(out=c_rstd, in_=c_rstd, func=AF.Exp, scale=-0.5),
                  reads=[sk], writes=[sk])

        def ln_apply(t):
            hv = HB[:, t, :]
            p8 = (t % 2) * 8
            c_rstd = stat[:, p8 + 3:p8 + 4]
            sk = ("stat", t % 2)
            s.add("dve", lambda e: e.scalar_tensor_tensor(out=hv, in0=hv, scalar=c_rstd, in1=LNG[:, :],
                                                          op0=ALU.mult, op1=ALU.mult),
                  reads=[sk, "lng", ("hb", t)], writes=[("hb", t)])
            s.add("dve", lambda e: e.tensor_tensor(out=hv, in0=hv, in1=LNB[:, :], op=ALU.add),
                  reads=["lnb", ("hb", t)], writes=[("hb", t)])
            dma("sp", y_d[t * 128:(t + 1) * 128, :], hv, reads=[("hb", t)], semkey="y%d" % t, final=True)

        load_wo(0)
        rot = 0
        banks2 = [0, 1, 2, 4, 5, 6, 7]
        for n8 in range(8):
            if n8 + 1 < 8:
                load_wo(n8 + 1)
            sl = n8 % 2
            for t in range(9):
                b = banks2[rot % len(banks2)]
                rot += 1
                for kc in range(32):
                    mx = MIXA[:, kc, t * 128:(t + 1) * 128] if kc < 16 else MIXB[:, kc - 16, t * 128:(t + 1) * 128]
                    s.add("pe", lambda e, kc=kc, mx=mx, b=b, sl=sl: e.matmul(pbank[b][:, 0:256], mx, WO[sl][:, kc, :],
                                                                             start=(kc == 0), stop=(kc == 31)),
                          reads=[("wo", sl, kc // 8)], writes=[bk(b)])
                hv8 = HB[:, t, n8 * 256:(n8 + 1) * 256]
                rs = rsum[:, t * 8 + n8:t * 8 + n8 + 1]
                s.add("dve", lambda e, hv8=hv8, b=b, rs=rs: e.scalar_tensor_tensor(out=hv8, in0=hv8, scalar=ALPHA,
                                                                                   in1=pbank[b][:, 0:256],
                                                                                   op0=ALU.mult, op1=ALU.add,
                                                                                   accum_out=rs),
                      reads=[bk(b), ("hb", t), "rsum"], writes=[("hb", t)])
                if n8 == 7:
                    if t >= 1:
                        ln_stats(t - 1)
                    if t >= 2:
                        ln_apply(t - 2)
        ln_stats(8)
        ln_apply(7)
        ln_apply(8)

        s.finalize(st)
        with nc.Block() as block:
            @block.tensor
            def _(e):
                s.emit("pe", e)

            @block.scalar
            def _(e):
                s.emit("act", e)

            @block.vector
            def _(e):
                s.emit("dve", e)

            @block.gpsimd
            def _(e):
                s.emit("pool", e)

            @block.sync
            def _(e):
                s.emit("sp", e)
    return nc


_NC_CACHE = {}


def _consts():
    idx = np.arange(128)
    identf = np.eye(128, dtype=np.float32)
    sidx = idx[:, None]
    tidx = idx[None, :]
    maskp = ((sidx <= tidx) & (sidx // 64 == tidx // 64)).astype(np.float32)
    masks = ((sidx <= tidx) & (sidx // 8 == tidx // 8)).astype(np.float32)
    mk64 = np.ones((128, 385), np.float32)
    mk64[:, 0::64] = 0.0
    mkb = np.ones((128, 385), np.float32)
    mkb[:, 0:256:64] = 0.0
    mkb[:, 256::8] = 0.0
    selcol = (idx[:, None] // 8 == np.arange(16)[None, :]).astype(np.float32)
    onesdiv = np.full((128, 128), 1.0 / 128.0, np.float32)
    return dict(identf=identf, maskp=maskp, masks=masks, mk64=mk64, mkb=mkb, selcol=selcol, onesdiv=onesdiv,
                mr64=(1.0 - mk64).astype(np.float32), mrb=(1.0 - mkb).astype(np.float32))


def kernel(x_prompt, x_sample, state_conv, state_hgrn, w_in, conv_w, norm_a, lb_logits, norm_b, w_out,
           ln_gain, ln_bias):
    f32 = np.float32
    x_prompt = np.asarray(x_prompt, f32)
    x_sample = np.asarray(x_sample, f32)
    state_conv = np.asarray(state_conv, f32)
    state_hgrn = np.asarray(state_hgrn, f32)
    w_in = np.asarray(w_in, f32)
    w_out = np.asarray(w_out, f32)
    if "nc" not in _NC_CACHE:
        _NC_CACHE["nc"] = build()
    nc = _NC_CACHE["nc"]

    w8 = w_in[0].reshape(D_MODEL, 8, 16, 128)
    wh = w8[:, 4:8].transpose(0, 2, 1, 3)
    wc = w8[:, 0:4].transpose(0, 2, 1, 3)
    wu = np.ascontiguousarray(np.concatenate([wh, wc], axis=1).reshape(D_MODEL, 32 * 512))
    wo = np.ascontiguousarray(w_out[0])

    def cm(vec):
        return np.ascontiguousarray(np.asarray(vec, f32).reshape(16, 128).T)

    cw = np.ascontiguousarray(np.asarray(conv_w, f32)[0].reshape(3, 16, 128).transpose(2, 1, 0).reshape(128, 48))
    na = cm(norm_a[0])
    nb = cm(norm_b[0])
    lbl = np.ascontiguousarray(np.concatenate([cm(lb_logits[0]), cm(lb_logits[1])], axis=1))
    lng = np.ascontiguousarray(np.broadcast_to(np.asarray(ln_gain, f32)[0][None, :], (128, D_MODEL)))
    lnb = np.ascontiguousarray(np.broadcast_to(np.asarray(ln_bias, f32)[0][None, :], (128, D_MODEL)))
    consts = _consts()

    in_maps = []
    for c in range(NCORES):
        b, hf = c // 2, c % 2
        xm = np.concatenate([x_prompt[b, hf * NPR:(hf + 1) * NPR],
                             x_sample[16 * c:16 * c + 16].reshape(NSM, D_MODEL)], axis=0)
        xp = x_prompt[b, 0:NPR] if hf == 1 else np.zeros((NPR, D_MODEL), f32)
        m = dict(xm=np.ascontiguousarray(xm), xp=np.ascontiguousarray(xp), wu=wu, wo=wo, cw=cw, na=na, nb=nb,
                 lbl=lbl, lng=lng, lnb=lnb,
                 scv=np.ascontiguousarray(state_conv[0, 16 * c:16 * c + 16].reshape(32, D_MODEL)),
                 shg=np.ascontiguousarray(state_hgrn[0, 16 * c:16 * c + 16]))
        m.update(consts)
        in_maps.append(m)

    res = run_bass_kernel_spmd(nc, in_maps, core_ids=list(range(NCORES)))
    R = res.results

    y_prompt = np.empty((4, 2048, D_MODEL), f32)
    y_sample = np.empty((128, 8, D_MODEL), f32)
    ncp = np.empty((1, 4, 2, 2048), f32)
    nhp = np.empty((1, 4, 16, 128, 128), f32)
    ncs = np.empty((1, 128, 2, 2048), f32)
    nhs = np.empty((1, 128, 16, 128, 128), f32)
    for c in range(NCORES):
        b, hf = c // 2, c % 2
        r = R[c]
        y_prompt[b, hf * NPR:(hf + 1) * NPR] = r["y"][0:NPR]
        y_sample[16 * c:16 * c + 16] = r["y"][NPR:NTOK].reshape(16, 8, D_MODEL)
        ncs[0, 16 * c:16 * c + 16] = r["ncv"][2:34].reshape(16, 2, 2048)
        nhs[0, 16 * c:16 * c + 16] = r["nhs"]
        if hf == 1:
            ncp[0, b] = r["ncv"][0:2]
            nhp[0, b] = r["nhp"]
    return (y_prompt, y_sample, ncp, nhp, ncs, nhs)
```

```python
import numpy as np
from contextlib import ExitStack
import concourse.bass as bass
import concourse.mybir as mybir
from concourse.bass_utils import run_bass_kernel_spmd

F32 = mybir.dt.float32
BF16 = mybir.dt.bfloat16
AF = mybir.ActivationFunctionType
ALU = mybir.AluOpType

D_MODEL = 2048
NCORES = 8
NPR = 1024
NSM = 128
NTOK = NPR + NSM
XC = NTOK + 2
EPS = 1e-5
QSCALE = 128.0 ** -0.5
ALPHA = 2.0 ** 0.25
TW = 388


class _Op:
    __slots__ = ("idx", "eng", "fn", "deps", "semkey", "signal", "ticket", "has_dep")

    def __init__(self, idx, eng, fn, deps, semkey):
        self.idx = idx
        self.eng = eng
        self.fn = fn
        self.deps = deps
        self.semkey = semkey
        self.signal = None
        self.ticket = None
        self.has_dep = False


class Sched:
    ENGS = ("pe", "act", "dve", "pool", "sp")

    def __init__(self, nc):
        self.nc = nc
        self.ops = []
        self.last_writer = {}
        self.readers = {}
        self.final_dma = []
        self.floor = None

    def add(self, eng, fn, reads=(), writes=(), semkey=None, final=False):
        deps = set()
        if self.floor is not None:
            deps.add(self.floor)
        for k in reads:
            w = self.last_writer.get(k)
            if w is not None:
                deps.add(w)
        for k in writes:
            w = self.last_writer.get(k)
            if w is not None:
                deps.add(w)
            for r in self.readers.get(k, ()):
                deps.add(r)
        idx = len(self.ops)
        op = _Op(idx, eng, fn, deps, semkey)
        self.ops.append(op)
        for k in reads:
            self.readers.setdefault(k, []).append(idx)
        for k in writes:
            self.last_writer[k] = idx
            self.readers[k] = []
        if final:
            self.final_dma.append(idx)
        return idx

    def barrier(self, eng, fn):
        deps = set()
        if self.floor is not None:
            deps.add(self.floor)
        for w in self.last_writer.values():
            deps.add(w)
        for rs in self.readers.values():
            deps.update(rs)
        idx = len(self.ops)
        op = _Op(idx, eng, fn, deps, None)
        self.ops.append(op)
        self.floor = idx
        self.last_writer = {}
        self.readers = {}
        return idx

    def finalize(self, stack):
        nc = self.nc
        ops = self.ops
        for op in ops:
            for d in op.deps:
                dop = ops[d]
                if dop.eng == "pe" and op.eng == "pe" and dop.semkey is None:
                    continue
                dop.has_dep = True
        for i in self.final_dma:
            ops[i].has_dep = True
        self.eng_sem = {}
        for e in self.ENGS:
            self.eng_sem[e] = stack.enter_context(nc.semaphore("s_" + e))
        self.dma_sem = {}
        cnt = {e: 0 for e in self.ENGS}
        dcnt = {}
        for op in ops:
            if op.semkey is not None:
                if op.semkey not in self.dma_sem:
                    self.dma_sem[op.semkey] = stack.enter_context(
                        nc.semaphore("d_" + str(op.semkey)))
                    dcnt[op.semkey] = 0
                dcnt[op.semkey] += 16
                op.signal = self.dma_sem[op.semkey]
                op.ticket = dcnt[op.semkey]
            elif op.has_dep:
                cnt[op.eng] += 1
                op.signal = self.eng_sem[op.eng]
                op.ticket = cnt[op.eng]
        self.n_sig = cnt

    def emit(self, eng, e):
        ops = self.ops
        waited = {}
        for op in ops:
            if op.eng != eng:
                continue
            need = {}
            for d in op.deps:
                dop = ops[d]
                if dop.signal is None:
                    continue
                if dop.eng == "pe" and eng == "pe" and dop.semkey is None:
                    continue
                key = id(dop.signal)
                if key not in need or need[key][1] < dop.ticket:
                    need[key] = (dop.signal, dop.ticket)
            for key, (sem, val) in need.items():
                if waited.get(key, 0) >= val:
                    continue
                e.wait_ge(sem, val)
                waited[key] = val
            ins = op.fn(e)
            if op.semkey is not None:
                ins.then_inc(op.signal, 16)
            elif op.signal is not None:
                ins.then_inc(op.signal, 1)
        if eng == "sp":
            need = {}
            for i in self.final_dma:
                dop = ops[i]
                key = id(dop.signal)
                if key not in need or need[key][1] < dop.ticket:
                    need[key] = (dop.signal, dop.ticket)
            for key, (sem, val) in need.items():
                e.wait_ge(sem, val)


def _interleave(a, b, lead=0):
    ia, ib = 0, 0
    while ia < min(lead, len(a)):
        a[ia]()
        ia += 1
    while ia < len(a) or ib < len(b):
        if ib < len(b):
            b[ib]()
            ib += 1
        if ia < len(a):
            a[ia]()
            ia += 1


class _Blk:
    def __init__(self):
        self.pgroups = []
        self.pre = None
        self.pieces = []
        self.fin = None


def build():
    nc = bass.Bass("TRN2", target_bir_lowering=False)

    def din(name, shape):
        return nc.dram_tensor(name, shape, F32, kind="ExternalInput").ap()

    def dout(name, shape):
        return nc.dram_tensor(name, shape, F32, kind="ExternalOutput").ap()

    xm_d = din("xm", [NTOK, D_MODEL])
    xp_d = din("xp", [NPR, D_MODEL])
    wu_d = din("wu", [D_MODEL, 32 * 512])
    wo_d = din("wo", [4096, D_MODEL])
    cw_d = din("cw", [128, 48])
    na_d = din("na", [128, 16])
    nb_d = din("nb", [128, 16])
    lbl_d = din("lbl", [128, 32])
    lng_d = din("lng", [128, D_MODEL])
    lnb_d = din("lnb", [128, D_MODEL])
    scv_d = din("scv", [32, D_MODEL])
    shg_d = din("shg", [16, 16, 128, 128])
    idf_d = din("identf", [128, 128])
    mkp_d = din("maskp", [128, 128])
    mks_d = din("masks", [128, 128])
    mk64_d = din("mk64", [128, 385])
    mkb_d = din("mkb", [128, 385])
    mr64_d = din("mr64", [128, 385])
    mrb_d = din("mrb", [128, 385])
    sel_d = din("selcol", [128, 16])
    ones_d = din("onesdiv", [128, 128])

    y_d = dout("y", [NTOK, D_MODEL])
    ncv_d = dout("ncv", [34, D_MODEL])
    nhp_d = dout("nhp", [16, 128, 128])
    nhs_d = dout("nhs", [16, 16, 128, 128])

    wu_v = wu_d.rearrange("(kc p) c -> p kc c", p=128)
    wo_v = wo_d.rearrange("(kc p) c -> p kc c", p=128)

    with ExitStack() as st:
        def T(name, shape, dt):
            return st.enter_context(nc.sbuf_tensor(name, shape, dt))

        def PS(name, shape, dt):
            return st.enter_context(nc.psum_tensor(name, shape, dt))

        BIGN = 126976
        BIG = T("big", [128, BIGN // 2], BF16)
        RREG = T("rreg", [128, 16 * NTOK], BF16)
        MIXB = T("mixb", [128, 16, NTOK], BF16)
        identb = T("identb", [128, 128], BF16)
        identf = T("identf_s", [128, 128], F32)
        maskp = T("maskp_s", [128, 128], F32)
        masks = T("masks_s", [128, 128], F32)
        mk64 = T("mk64_s", [128, 385], F32)
        mkb = T("mkb_s", [128, 385], F32)
        selcol = T("selcol_s", [128, 16], F32)
        selb = T("selb_s", [128, 16], BF16)
        mr64 = T("mr64_s", [128, 385], F32)
        mrb = T("mrb_s", [128, 385], F32)
        rowm = T("rowm_s", [128, 2], F32)
        hc0 = T("hc0_s", [128, 16], F32)
        hc1 = T("hc1_s", [128, 16], F32)
        hnc1 = T("hnc1_s", [128, 16], F32)
        onesdiv = T("onesdiv_s", [128, 128], F32)
        cw = T("cw_s", [128, 48], F32)
        na = T("na_s", [128, 16], F32)
        nb = T("nb_s", [128, 16], F32)
        lbl = T("lbl_s", [128, 32], F32)
        lb = T("lb_s", [128, 16], F32)
        oml = T("oml_s", [128, 16], F32)
        noml = T("noml_s", [128, 16], F32)
        BL = T("bl_s", [128, 24], F32)
        DL = T("dl_s", [128, 24], F32)
        dummy = T("dummy_s", [128, 4], F32)
        stat = T("stat_s", [128, 8], F32)
        SCG = T("scg_s", [32, 128], F32)

        xprevT = RREG[:, 0:16 * NPR].rearrange("p (a b) -> p a b", a=16)
        MIXA = RREG[:, :].rearrange("p (a b) -> p a b", a=16)

        pbank = {}
        for i in (0, 1, 2, 4, 5, 6, 7):
            pbank[i] = PS("pb%d" % i, [128, 512], F32)
        pT = PS("pT", [128, 1024], BF16)

        def carve(off, shape, dt):
            n = 1
            for d in shape[1:]:
                n *= d
            esz = 2 if dt == BF16 else 4
            assert off % 4 == 0
            v = BIG[:, off // 2: off // 2 + n * esz // 2]
            if dt == F32:
                v = v.bitcast(F32)
            if len(shape) == 3:
                v = v.rearrange("p (a b) -> p a b", a=shape[1])
            return v

        off = 0
        xT = carve(off, [128, 16, XC], BF16); off += 16 * XC * 2
        WSL = []
        for sl in range(2):
            WSL.append(carve(off, [128, 16, 512], BF16)); off += 16384
        STRAW = off; off += 8192
        XS = [carve(STRAW, [128, 2048], BF16), carve(STRAW + 4096, [128, 2048], BF16)]
        S0 = carve(STRAW, [128, 16, 128], F32)
        CVO = carve(STRAW, [128, 2048], F32)
        ET = [[None] * 3 for _ in range(2)]
        for par in range(2):
            for i in range(3):
                ET[par][i] = carve(off, [128, TW], F32); off += TW * 4
        G = []
        for i in range(7):
            G.append(carve(off, [128, TW], F32)); off += TW * 4
        QE, KE, KTC, DLT = [], [], [], []
        for par in range(2):
            QE.append(carve(off, [128, TW], BF16)); off += TW * 2
            KE.append(carve(off, [128, TW], BF16)); off += TW * 2
            KTC.append(carve(off, [128, TW], BF16)); off += TW * 2
            DLT.append(carve(off, [128, 24], F32)); off += 96
        VCM = carve(off, [128, TW], BF16); off += TW * 2
        VT = []
        for par in range(2):
            VT.append(carve(off, [128, 384], BF16)); off += 768
        KT = []
        SCM = []
        for i in range(3):
            KT.append(carve(off, [128, 128], BF16)); off += 256
        for i in range(3):
            SCM.append(carve(off, [128, 128], BF16)); off += 256
        SST = [[None, None], [None, None]]
        for i in range(2):
            for r in range(2):
                SST[i][r] = carve(off, [128, 128], F32); off += 512
        NSBF = 6
        SBF = []
        for i in range(NSBF):
            SBF.append(carve(off, [128, 128], BF16)); off += 256
        HG_OVL = off
        S0bf = carve(off, [128, 16, 128], BF16); off += 4096
        VBLK = carve(off, [128, 16, 128], BF16); off += 4096
        UB = carve(HG_OVL, [128, XC], F32)
        USM = carve(HG_OVL + XC * 4, [128, 16, 10], F32)
        UO = carve(HG_OVL + XC * 4 + 640, [128, 16, 34], F32)
        off = max(off, HG_OVL + XC * 4 + 640 + 16 * 34 * 4)
        V2 = []
        IDL = []
        for par in range(2):
            V2.append(carve(off, [128, 3 * 256], BF16)); off += 1536
            IDL.append(carve(off, [128, 24], F32)); off += 96
        assert off <= BIGN, off
        STK = [("XS", 0), ("XS", 1)]

        HB = carve(0, [128, 9, D_MODEL], F32)
        WO = [carve(73728, [128, 32, 256], BF16), carve(73728 + 16384, [128, 32, 256], BF16)]
        LNG = carve(106496, [128, D_MODEL], F32)
        LNB = carve(106496 + 8192, [128, D_MODEL], F32)

        s = Sched(nc)

        def dma(q, out, in_, reads=(), writes=(), semkey=None, final=False):
            return s.add(q, lambda e: e.dma_start(out=out, in_=in_), reads=reads, writes=writes,
                         semkey=semkey, final=final)

        def bk(i):
            return ("bank", i)

        for dst, src, key in [(identf, idf_d, "identf"), (maskp, mkp_d, "maskp"), (masks, mks_d, "masks"),
                              (mk64, mk64_d, "mk64"), (mkb, mkb_d, "mkb"), (selcol, sel_d, "selcol"),
                              (mr64, mr64_d, "mr64"), (mrb, mrb_d, "mrb"),
                              (onesdiv, ones_d, "onesdiv"), (cw, cw_d, "cw"), (na, na_d, "na"),
                              (nb, nb_d, "nb"), (lbl, lbl_d, "lbl")]:
            dma("sp", dst[:], src[:, :], writes=[key], semkey="c_" + key)
        dma("pool", identb[:], idf_d[:, :], writes=["identb"], semkey="c_identb")
        dma("pool", selb[:], sel_d[:, :], writes=["selb"], semkey="c_selb")
        s.add("dve", lambda e: e.reduce_sum(out=rowm[:, 0:1], in_=selcol[:, 0:8], axis=mybir.AxisListType.X),
              reads=["selcol"], writes=["rowm"])
        s.add("dve", lambda e: e.reduce_sum(out=rowm[:, 1:2], in_=selcol[:, 8:16], axis=mybir.AxisListType.X),
              reads=["selcol"], writes=["rowm"])
        s.add("dve", lambda e: e.memset(G[0][:], 0.5), writes=["G0"])
        s.add("dve", lambda e: e.memset(dummy[:], 0.0), writes=["dummy"])

        s.add("act", lambda e: e.activation(out=lbl[:], in_=lbl[:], func=AF.Exp), reads=["lbl"], writes=["lbl"])
        s.add("dve", lambda e: e.tensor_tensor(out=oml[:], in0=lbl[:, 0:16], in1=lbl[:, 16:32], op=ALU.add),
              reads=["lbl"], writes=["oml"])
        s.add("dve", lambda e: e.reciprocal(out=oml[:], in_=oml[:]), reads=["oml"], writes=["oml"])
        s.add("dve", lambda e: e.tensor_tensor(out=lb[:], in0=lbl[:, 0:16], in1=oml[:], op=ALU.mult),
              reads=["lbl", "oml"], writes=["lb"])
        s.add("dve", lambda e: e.tensor_scalar(out=oml[:], in0=lb[:], scalar1=-1.0, scalar2=1.0,
                                               op0=ALU.mult, op1=ALU.add), reads=["lb"], writes=["oml"])
        s.add("dve", lambda e: e.tensor_scalar(out=noml[:], in0=oml[:], scalar1=-1.0, scalar2=None,
                                               op0=ALU.mult), reads=["oml"], writes=["noml"])
        s.add("dve", lambda e: e.tensor_scalar(out=hc1[:], in0=oml[:], scalar1=0.5, scalar2=None, op0=ALU.mult),
              reads=["oml"], writes=["hc1"])
        s.add("dve", lambda e: e.tensor_scalar(out=hnc1[:], in0=oml[:], scalar1=-0.5, scalar2=None, op0=ALU.mult),
              reads=["oml"], writes=["hnc1"])
        s.add("dve", lambda e: e.tensor_tensor(out=hc0[:], in0=lb[:], in1=hc1[:], op=ALU.add),
              reads=["lb", "hc1"], writes=["hc0"])

        def load_unit_weights(u):
            sl = u % 2
            for pc in range(4):
                dma("pool", WSL[sl][:, pc * 4:(pc + 1) * 4, :], wu_v[:, pc * 4:(pc + 1) * 4, u * 512:(u + 1) * 512],
                    writes=[("w", sl, pc)], semkey="w%d_%d" % (sl, pc))

        def wkeys(sl):
            return [("w", sl, pc) for pc in range(4)]

        load_unit_weights(0)

        tcount = [0]

        XTK = [("xT", t) for t in range(9)] + ["xTh"]
        XPK = [("xp", t) for t in range(8)]

        XS4 = [XS[0], XS[1], S0bf[:, :, :].rearrange("p a b -> p (a b)"), VBLK[:, :, :].rearrange("p a b -> p (a b)")]
        XSK = [("XS", 0), ("XS", 1), "S0bf", "VBLK"]

        def load_xT(src_rows, dstT, col0, wkey):
            t = tcount[0]
            tcount[0] += 1
            par = t % 4
            xs_ap = XS4[par]
            xs_key = XSK[par]
            dma("pool", xs_ap[:], src_rows, writes=[xs_key], semkey="xs%d" % par)
            for half in range(2):
                if (2 * t + half) % 2 == 0:
                    bank_ap = pT[:, :]
                    bkey = "pT"
                else:
                    bank_ap = pbank[0][:].bitcast(BF16)
                    bkey = bk(0)
                for k8 in range(8):
                    kc = half * 8 + k8
                    s.add("pe", lambda e, kc=kc, k8=k8, bank_ap=bank_ap, xs_ap=xs_ap: e.transpose(
                        bank_ap[:, k8 * 128:(k8 + 1) * 128], xs_ap[:, kc * 128:(kc + 1) * 128], identb[:]),
                        reads=[xs_key, "identb"], writes=[bkey])
                eng = "act" if half == 0 else "dve"
                dst = dstT[:, half * 8:(half + 1) * 8, col0:col0 + 128]
                src = bank_ap.rearrange("p (a b) -> p a b", a=8)
                if eng == "act":
                    s.add("act", lambda e, dst=dst, src=src: e.copy(out=dst, in_=src), reads=[bkey], writes=[wkey])
                else:
                    s.add("dve", lambda e, dst=dst, src=src: e.tensor_copy(out=dst, in_=src), reads=[bkey], writes=[wkey])

        for t in range(8):
            load_xT(xp_d[t * 128:(t + 1) * 128, :], xprevT, t * 128, ("xp", t))
        s.add("dve", lambda e: e.tensor_copy(out=xT[:, :, 0:2], in_=xprevT[:, :, NPR - 2:NPR]),
              reads=XPK, writes=["xTh"])

        proj_rot = {"banks": [0, 1], "i": 0}

        def next_bank():
            b = proj_rot["banks"][proj_rot["i"] % len(proj_rot["banks"])]
            proj_rot["i"] += 1
            return b

        def proj_cm(u, j, src, c0, n, bank):
            sl = u % 2
            for kc in range(16):
                s.add("pe", lambda e, kc=kc: e.matmul(pbank[bank][:, 0:n], WSL[sl][:, kc, j * 128:(j + 1) * 128],
                                                       src[:, kc, c0:c0 + n], start=(kc == 0), stop=(kc == 15)),
                      reads=[("w", sl, kc // 4)] + (XTK if src is xT else XPK), writes=[bk(bank)])

        def proj_tm(u, j, src, c0, ntile):
            sl = u % 2
            for tau in range(ntile):
                for kc in range(16):
                    s.add("pe", lambda e, kc=kc, tau=tau: e.matmul(
                        pbank[4][:, tau * 128:(tau + 1) * 128], src[:, kc, c0 + tau * 128:c0 + (tau + 1) * 128],
                        WSL[sl][:, kc, j * 128:(j + 1) * 128], start=(kc == 0), stop=(kc == 15)),
                        reads=[("w", sl, kc // 4)] + (XTK if src is xT else XPK), writes=[bk(4)])

        DSB = [pbank[7][:, 0:128], pbank[4][:, 384:512]]
        DSK = [bk(7), bk(4)]
        blk_count = [0]
        tile_ctr = [0]
        ds_ctr = [0]
        snap_ctr = [0]
        hstates = [{"sidx": 0} for _ in range(17)]

        def make_hgrn_block(h, main, bi):
            u = h
            blk = _Blk()
            par = blk_count[0] % 2
            blk_count[0] += 1
            hp = h % 2
            if main:
                src = xT
                c0 = 2 + 384 * bi
                n = 384
                segs = [(0, 6, 64)] if bi < 2 else [(0, 4, 64), (256, 16, 8)]
                tiles = ["P", "P", "P"] if bi < 2 else ["P", "P", "S"]
                m0, m1 = (mk64, mr64) if bi < 2 else (mkb, mrb)
                tok0 = 384 * bi
            else:
                src = xprevT
                c0 = 384 * bi
                n = 384 if bi < 2 else 256
                segs = [(0, n // 64, 64)]
                tiles = ["P"] * (n // 128)
                m0, m1 = mk64, mr64
                tok0 = None
            ntile = len(tiles)
            Eth, Eq, Esz = ET[par]
            V = VT[par]
            QEp, KEp, KTCp, DLp = QE[par], KE[par], KTC[par], DLT[par]
            hs = slice(h, h + 1)
            dl_idx = {}
            chain_state = {}
            kq = ("QE", par)
            kk_ = ("KE", par)
            kkt = ("KTC", par)
            kdl = ("DL", par)
            kidl = ("IDL", par)
            IDLp = IDL[par]
            V2p = V2[par]
            hst = hstates[h]

            def g_f():
                b = next_bank()
                proj_cm(u, 1, src, c0, n, b)
                s.add("act", lambda e: e.activation(out=Eth[:, 0:n], in_=pbank[b][:, 0:n], func=AF.Tanh, scale=0.5),
                      reads=[bk(b)], writes=[("E", par, 0)])
                s.add("act", lambda e: e.activation(out=G[0][:, 0:n], in_=Eth[:, 0:n], func=AF.Identity,
                                                    scale=hc1[:, hs], bias=hc0[:, hs]),
                      reads=[("E", par, 0), "hc0", "hc1"], writes=["G0"])
                s.add("act", lambda e: e.activation(out=G[1][:, 0:n], in_=Eth[:, 0:n], func=AF.Identity,
                                                    scale=hnc1[:, hs], bias=hc1[:, hs]),
                      reads=[("E", par, 0), "hnc1", "hc1"], writes=["G1"])
                def rev(ap2d):
                    (ps_, pn_), (fs_, fn_) = ap2d.ap
                    return bass.AP(ap2d.tensor, ap2d.offset + (fn_ - 1) * fs_, [[ps_, pn_], [-fs_, fn_]])

                s.add("dve", lambda e: e.tensor_tensor(out=G[2][:, 0:n], in0=G[0][:, 1:n + 1], in1=m0[:, 1:n + 1],
                                                       op=ALU.mult), reads=["G0", "mk64", "mkb"], writes=["G2"])
                s.add("dve", lambda e: e.tensor_tensor_scan(out=rev(G[3][:, 0:n]), data0=rev(G[2][:, 0:n]),
                                                            data1=rev(m1[:, 1:n + 1]), initial=0.0,
                                                            op0=ALU.mult, op1=ALU.add),
                      reads=["G2", "mr64", "mrb"], writes=["G3"])
                if main:
                    s.add("dve", lambda e: e.tensor_tensor(out=G[2][:, 0:n], in0=G[1][:, 0:n], in1=G[3][:, 0:n],
                                                           op=ALU.mult), reads=["G1", "G3"], writes=["G2"])
                else:
                    s.add("dve", lambda e: e.tensor_tensor(out=KTCp[:, 0:n], in0=G[1][:, 0:n], in1=G[3][:, 0:n],
                                                           op=ALU.mult), reads=["G1", "G3"], writes=[kkt])
                o = 0
                for (sc0, nch, L) in segs:
                    fv = G[0][:, sc0:sc0 + nch * L].rearrange("p (c t) -> p c t", t=L)[:, :, 0]
                    rv = G[3][:, sc0:sc0 + nch * L].rearrange("p (c t) -> p c t", t=L)[:, :, 0]
                    s.add("dve", lambda e, fv=fv, rv=rv, o=o, nch=nch: e.tensor_tensor(
                        out=DLp[:, o:o + nch], in0=fv, in1=rv, op=ALU.mult), reads=["G0", "G3"], writes=[kdl])
                    for ci in range(nch):
                        dl_idx[(sc0, ci)] = o + ci
                    o += nch
                chain_state['ntot'] = o

            def g_q():
                b = next_bank()
                proj_cm(u, 0, src, c0, n, b)
                s.add("act", lambda e: e.activation(out=Eq[:, 0:n], in_=pbank[b][:, 0:n], func=AF.Silu),
                      reads=[bk(b)], writes=[("E", par, 1)])
                ntot = chain_state['ntot']
                if main:
                    s.add("act", lambda e: e.copy(out=KTCp[:, 0:n], in_=G[2][:, 0:n]), reads=["G2"], writes=[kkt])
                    s.add("dve", lambda e: e.reciprocal(out=IDLp[:, 0:ntot], in_=DLp[:, 0:ntot]),
                          reads=[kdl], writes=[kidl])
                    o = 0
                    for (sc0, nch, L) in segs:
                        src3 = G[2][:, sc0:sc0 + nch * L].rearrange("p (c t) -> p c t", t=L)
                        dst3 = KEp[:, sc0:sc0 + nch * L].rearrange("p (c t) -> p c t", t=L)
                        dlb = IDLp[:, o:o + nch].unsqueeze(2).to_broadcast([128, nch, L])
                        s.add("dve", lambda e, src3=src3, dst3=dst3, dlb=dlb: e.tensor_tensor(
                            out=dst3, in0=src3, in1=dlb, op=ALU.mult), reads=["G2", kidl], writes=[kk_])
                        o += nch

                if main:
                    s.add("dve", lambda e: e.tensor_tensor(out=G[3][:, 0:n], in0=G[0][:, 0:n], in1=m0[:, 0:n],
                                                           op=ALU.mult), reads=["G0", "mk64", "mkb"], writes=["G3"])
                    s.add("dve", lambda e: e.tensor_tensor(out=G[2][:, 0:n], in0=G[0][:, 0:n], in1=m1[:, 0:n],
                                                           op=ALU.mult), reads=["G0", "mr64", "mrb"], writes=["G2"])
                    s.add("dve", lambda e: e.tensor_tensor_scan(out=G[4][:, 0:n], data0=G[3][:, 0:n],
                                                                data1=G[2][:, 0:n], initial=0.0,
                                                                op0=ALU.mult, op1=ALU.add),
                          reads=["G2", "G3"], writes=["G4"])
                s.add("dve", lambda e: e.scalar_tensor_tensor(out=QEp[:, 0:n], in0=Eq[:, 0:n], scalar=QSCALE,
                                                              in1=G[4][:, 0:n], op0=ALU.mult, op1=ALU.mult),
                      reads=[("E", par, 1), "G4"], writes=[kq])

            def g_v():
                b = next_bank()
                proj_cm(u, 2, src, c0, n, b)
                s.add("act", lambda e: e.copy(out=VCM[:, 0:n], in_=pbank[b][:, 0:n]), reads=[bk(b)], writes=["VCM"])

            def g_vt():
                aux16 = pbank[4][:].bitcast(BF16)
                for tau in range(ntile):
                    s.add("pe", lambda e, tau=tau: e.transpose(aux16[:, tau * 128:(tau + 1) * 128],
                                                               VCM[:, tau * 128:(tau + 1) * 128], identb[:]),
                          reads=["VCM", "identb"], writes=[bk(4)])
                s.add("act", lambda e: e.copy(out=V[:, 0:n], in_=aux16[:, 0:n]), reads=[bk(4)], writes=[("V", par)])
                if main and bi == 2:
                    s.add("dve", lambda e: e.tensor_tensor(
                        out=VBLK[:, :, :], in0=V[:, 256:384].unsqueeze(1).to_broadcast([128, 16, 128]),
                        in1=selb[:, :].unsqueeze(2).to_broadcast([128, 16, 128]), op=ALU.mult),
                        reads=[("V", par), "selb"], writes=["VBLK"])
                for c in range(2):
                    s.add("act", lambda e, c=c: e.activation(
                        out=V2p[:, 0:ntile * 256].rearrange("p (t c v) -> p t c v", c=2, v=128)[:, :, c, :],
                        in_=aux16[:, 0:n].rearrange("p (t v) -> p t v", v=128), func=AF.Identity,
                        scale=rowm[:, c:c + 1]), reads=[bk(4), "rowm"], writes=[("V2", par)])

            def g_z():
                g_vt()
                b = next_bank()
                proj_cm(u, 3, src, c0, n, b)
                s.add("act", lambda e: e.activation(out=Esz[:, 0:n], in_=pbank[b][:, 0:n], func=AF.Silu),
                      reads=[bk(b)], writes=[("E", par, 2)])

            blk.pgroups = [g_f, g_q, g_v, g_z] if main else [g_f, g_v, g_vt]

            snaps = {}
            scm_slot = {}
            kt_slot = {}

            def make_T(tau):
                def T_():
                    tc = slice(tau * 128, (tau + 1) * 128)
                    kt = tile_ctr[0] % 3
                    tile_ctr[0] += 1
                    kt_slot[tau] = kt
                    scm_slot[tau] = kt
                    s.add("pe", lambda e: e.transpose(pT[:, 0:128], KTCp[:, tc], identb[:]),
                          reads=[kkt, "identb"], writes=["pT"])
                    s.add("act", lambda e: e.copy(out=KT[kt][:], in_=pT[:, 0:128]), reads=["pT"], writes=[("KT", kt)])
                    if main:
                        s.add("pe", lambda e: e.matmul(pbank[5][:, 0:128], KEp[:, tc], QEp[:, tc], start=True, stop=True),
                              reads=[kk_, kq], writes=[bk(5)])
                        mk = maskp if tiles[tau] == "P" else masks
                        s.add("dve", lambda e: e.tensor_tensor(out=SCM[kt][:], in0=pbank[5][:, 0:128], in1=mk[:],
                                                               op=ALU.mult),
                              reads=[bk(5), "maskp", "masks"], writes=[("SCM", kt)])
                return T_

            def make_D(tau):
                def D_():
                    tc = slice(tau * 128, (tau + 1) * 128)
                    kt = kt_slot[tau]
                    if tiles[tau] == "P":
                        dsb = 7 if ds_ctr[0] % 2 == 0 else 2
                        ds_ctr[0] += 1
                        s.add("pe", lambda e: e.matmul(pbank[dsb][:, 0:256], KT[kt][:],
                                                       V2p[:, tau * 256:(tau + 1) * 256], start=True, stop=True),
                              reads=[("KT", kt), ("V2", par)], writes=[bk(dsb)])
                        for ci in range(2):
                            cur = hst["sidx"] % 2
                            nxt = (hst["sidx"] + 1) % 2
                            hst["sidx"] += 1
                            if main:
                                sb = snap_ctr[0] % NSBF
                                snap_ctr[0] += 1
                                snaps[(tau, ci)] = sb
                                s.add("dve", lambda e, sb=sb, cur=cur: e.tensor_copy(out=SBF[sb][:],
                                                                                      in_=SST[hp][cur][:]),
                                      reads=[("SST", hp, cur)], writes=[("SBF", sb)])
                            di = dl_idx[(0, tau * 2 + ci)]
                            s.add("dve", lambda e, di=di, ci=ci, cur=cur, nxt=nxt: e.scalar_tensor_tensor(
                                out=SST[hp][nxt][:], in0=SST[hp][cur][:], scalar=DLp[:, di:di + 1],
                                in1=pbank[dsb][:, ci * 128:(ci + 1) * 128], op0=ALU.mult, op1=ALU.add),
                                reads=[("SST", hp, cur), kdl, bk(dsb)], writes=[("SST", hp, nxt)])
                    else:
                        for q in range(4):
                            db = 7 if q % 2 == 0 else 5
                            s.add("pe", lambda e, q=q, db=db: e.matmul(
                                pbank[db][:, 0:512], KT[kt][:],
                                VBLK[:, :, :].rearrange("p a b -> p (a b)")[:, 512 * q:512 * (q + 1)],
                                start=True, stop=True),
                                reads=[("KT", kt), "VBLK"], writes=[bk(db)])
                            di = dl_idx[(256, 4 * q)]
                            s0q = S0[:, 4 * q:4 * q + 4, :]
                            s.add("dve", lambda e, s0q=s0q, di=di: e.tensor_tensor(
                                out=s0q, in0=s0q, in1=DLp[:, di:di + 4].unsqueeze(2).to_broadcast([128, 4, 128]),
                                op=ALU.mult), reads=STK + [kdl], writes=STK)
                            s.add("dve", lambda e, s0q=s0q, db=db: e.tensor_tensor(
                                out=s0q, in0=s0q, in1=pbank[db][:, 0:512].rearrange("p (j v) -> p j v", j=4),
                                op=ALU.add), reads=STK + [bk(db)], writes=STK)
                        dma("sp", nhs_d[:, h, :, :].rearrange("j k v -> k j v"), S0[:, :, :], reads=STK,
                            semkey="nhs", final=True)
                return D_

            def make_B(tau):
                def B_():
                    tc = slice(tau * 128, (tau + 1) * 128)
                    sc = scm_slot[tau]
                    s.add("pe", lambda e: e.matmul(pbank[6][:, 0:128], V[:, tc], SCM[sc][:], start=True, stop=False),
                          reads=[("V", par), ("SCM", sc)], writes=[bk(6)])
                    if tiles[tau] == "P":
                        for ci in range(2):
                            sb = snaps[(tau, ci)]
                            qc = slice(tau * 128 + ci * 64, tau * 128 + (ci + 1) * 64)
                            s.add("pe", lambda e, qc=qc, sb=sb, ci=ci: e.matmul(
                                pbank[6][:, ci * 64:(ci + 1) * 64], SBF[sb][:], QEp[:, qc],
                                start=False, stop=(ci == 1)),
                                reads=[("SBF", sb), kq], writes=[bk(6)])
                    else:
                        for j in range(16):
                            qc = slice(tau * 128 + 8 * j, tau * 128 + 8 * j + 8)
                            s.add("pe", lambda e, j=j, qc=qc: e.matmul(
                                pbank[6][:, 8 * j:8 * j + 8], S0bf[:, j, :], QEp[:, qc], start=False, stop=(j == 15)),
                                reads=["S0bf", kq], writes=[bk(6)])
                    s.add("act", lambda e: e.copy(out=G[5][:, tc], in_=pbank[6][:, 0:128]),
                          reads=[bk(6)], writes=["G5"])
                return B_

            nst = ntile + (2 if main else 1)
            pcs = []
            for i in range(nst):
                parts = []
                if i < ntile:
                    parts.append(make_T(i))
                if 0 <= i - 1 < ntile:
                    parts.append(make_D(i - 1))
                if main and 0 <= i - 2 < ntile:
                    parts.append(make_B(i - 2))
                pcs.append(lambda parts=parts: [p() for p in parts])
            blk.pieces = pcs

            def fin():
                s.add("act", lambda e: e.activation(out=G[6][:, 0:n], in_=G[5][:, 0:n], func=AF.Square),
                      reads=["G5"], writes=["G6"])
                s.add("pe", lambda e: e.matmul(pbank[4][:, 0:n], onesdiv[:], G[6][:, 0:n], start=True, stop=True),
                      reads=["onesdiv", "G6"], writes=[bk(4)])
                s.add("act", lambda e: e.activation(out=G[6][:, 0:n], in_=pbank[4][:, 0:n], func=AF.Ln, bias=EPS),
                      reads=[bk(4)], writes=["G6"])
                s.add("act", lambda e: e.activation(out=G[6][:, 0:n], in_=G[6][:, 0:n], func=AF.Exp, scale=-0.5),
                      reads=["G6"], writes=["G6"])
                s.add("dve", lambda e: e.tensor_tensor(out=G[5][:, 0:n], in0=G[5][:, 0:n], in1=G[6][:, 0:n], op=ALU.mult),
                      reads=["G5", "G6"], writes=["G5"])
                s.add("dve", lambda e: e.scalar_tensor_tensor(out=MIXB[:, h, tok0:tok0 + n], in0=G[5][:, 0:n],
                                                              scalar=nb[:, hs], in1=Esz[:, 0:n],
                                                              op0=ALU.mult, op1=ALU.mult),
                      reads=["G5", "nb", ("E", par, 2)], writes=[("mixb", h)])
                if bi == 2:
                    cur = hst["sidx"] % 2
                    dma("sp", nhp_d[h, :, :], SST[hp][cur][:], reads=[("SST", hp, cur)], semkey="nhp%d" % hp,
                        final=True)

            blk.fin = fin if main else None
            return blk

        def make_conv_block(g, bi):
            u = 16 + g
            blk = _Blk()
            par = blk_count[0] % 2
            blk_count[0] += 1
            c0 = 0 if bi == 0 else 2 + 384 * bi
            n = 386 if bi == 0 else 384
            t_off = 2 if bi == 0 else 0
            tok0 = 384 * bi
            a0 = 2 + tok0
            npr = 384 if bi < 2 else 256
            Ev, Eb, Esz = ET[par]

            def g_v():
                b = next_bank()
                proj_cm(u, 0, xT, c0, n, b)
                s.add("act", lambda e: e.copy(out=Ev[:, 0:n], in_=pbank[b][:, 0:n]), reads=[bk(b)],
                      writes=[("E", par, 0)])

            def g_c():
                b = next_bank()
                proj_cm(u, 2, xT, c0, n, b)
                s.add("dve", lambda e: e.tensor_tensor(out=UB[:, c0:c0 + n], in0=pbank[b][:, 0:n], in1=Ev[:, 0:n],
                                                       op=ALU.mult),
                      reads=[bk(b), ("E", par, 0)], writes=["UB"])

            def g_b():
                b = next_bank()
                proj_cm(u, 1, xT, c0, n, b)
                s.add("act", lambda e: e.copy(out=Eb[:, 0:n], in_=pbank[b][:, 0:n]), reads=[bk(b)],
                      writes=[("E", par, 1)])

            def g_z():
                b = next_bank()
                proj_cm(u, 3, xT, c0, n, b)
                s.add("act", lambda e: e.activation(out=Esz[:, 0:n], in_=pbank[b][:, 0:n], func=AF.Silu),
                      reads=[bk(b)], writes=[("E", par, 2)])

            blk.pgroups = [g_v, g_c, g_b, g_z]

            def w(jj):
                return cw[:, g * 3 + jj:g * 3 + jj + 1]

            def c1():
                YC = G[0]
                s.add("dve", lambda e: e.tensor_scalar(out=YC[:, 0:npr], in0=UB[:, a0:a0 + npr], scalar1=w(2),
                                                       scalar2=None, op0=ALU.mult),
                      reads=["UB", "cw"], writes=["G0"])
                for jj, sh in ((1, 1), (0, 2)):
                    s.add("dve", lambda e, jj=jj, sh=sh: e.scalar_tensor_tensor(
                        out=YC[:, 0:npr], in0=UB[:, a0 - sh:a0 - sh + npr], scalar=w(jj), in1=YC[:, 0:npr],
                        op0=ALU.mult, op1=ALU.add), reads=["UB", "cw", "G0"], writes=["G0"])
                if bi == 2:
                    dma("sp", SCG[:], scv_d[:, g * 128:(g + 1) * 128], writes=["SCG"], semkey="scg")
                    s.add("pe", lambda e: e.transpose(pbank[4][:, 0:32], SCG[:], identf[0:32, 0:32]),
                          reads=["SCG", "identf"], writes=[bk(4)])
                    s.add("dve", lambda e: e.tensor_copy(
                        out=USM[:, :, 0:2], in_=pbank[4][:, 0:32].rearrange("p (j r) -> p j r", r=2)),
                        reads=[bk(4)], writes=["USM"])
                    s.add("dve", lambda e: e.tensor_copy(
                        out=USM[:, :, 2:10], in_=UB[:, 2 + NPR:2 + NTOK].rearrange("p (j t) -> p j t", t=8)),
                        reads=["UB"], writes=["USM"])
                    YS = YC[:, 256:384].rearrange("p (j t) -> p j t", t=8)
                    s.add("dve", lambda e: e.tensor_scalar(out=YS, in0=USM[:, :, 2:10], scalar1=w(2), scalar2=None,
                                                           op0=ALU.mult), reads=["USM", "cw"], writes=["G0"])
                    for jj, sh in ((1, 1), (0, 2)):
                        s.add("dve", lambda e, jj=jj, sh=sh: e.scalar_tensor_tensor(
                            out=YS, in0=USM[:, :, 2 - sh:10 - sh], scalar=w(jj), in1=YS,
                            op0=ALU.mult, op1=ALU.add), reads=["USM", "cw", "G0"], writes=["G0"])
                    s.add("dve", lambda e: e.tensor_copy(out=UO[:, g, 0:2], in_=UB[:, NPR:NPR + 2]),
                          reads=["UB"], writes=["UO"])
                    s.add("dve", lambda e: e.tensor_copy(
                        out=UO[:, g, 2:34].rearrange("p (j r) -> p j r", r=2), in_=USM[:, :, 8:10]),
                        reads=["USM"], writes=["UO"])
                s.add("dve", lambda e: e.tensor_tensor(out=G[1][:, 0:384], in0=Eb[:, t_off:t_off + 384],
                                                       in1=YC[:, 0:384], op=ALU.mult),
                      reads=[("E", par, 1), "G0"], writes=["G1"])
                s.add("act", lambda e: e.activation(out=G[2][:, 0:384], in_=G[1][:, 0:384], func=AF.Square),
                      reads=["G1"], writes=["G2"])

            def c2():
                gs = slice(g, g + 1)
                s.add("pe", lambda e: e.matmul(pbank[4][:, 0:384], onesdiv[:], G[2][:, 0:384], start=True, stop=True),
                      reads=["onesdiv", "G2"], writes=[bk(4)])
                s.add("act", lambda e: e.activation(out=G[2][:, 0:384], in_=pbank[4][:, 0:384], func=AF.Ln, bias=EPS),
                      reads=[bk(4)], writes=["G2"])
                s.add("act", lambda e: e.activation(out=G[2][:, 0:384], in_=G[2][:, 0:384], func=AF.Exp, scale=-0.5),
                      reads=["G2"], writes=["G2"])
                s.add("dve", lambda e: e.tensor_tensor(out=G[1][:, 0:384], in0=G[1][:, 0:384], in1=G[2][:, 0:384],
                                                       op=ALU.mult), reads=["G1", "G2"], writes=["G1"])
                s.add("dve", lambda e: e.scalar_tensor_tensor(out=MIXA[:, g, tok0:tok0 + 384], in0=G[1][:, 0:384],
                                                              scalar=na[:, gs], in1=Esz[:, t_off:t_off + 384],
                                                              op0=ALU.mult, op1=ALU.mult),
                      reads=["G1", "na", ("E", par, 2)], writes=[("mixa", g)])

            blk.pieces = [c1, c2]
            return blk

        LEAD = 0

        def run_block(blk, prev, init=None):
            _interleave(blk.pgroups, prev.pieces if prev is not None else [], lead=LEAD)
            if prev is not None and prev.fin is not None:
                prev.fin()
            if init is not None:
                init()
            return blk

        def run_blocks(blocks, prev):
            for blk in blocks:
                prev = run_block(blk, prev)
            return prev

        def init_state(h):
            hp = h % 2
            hstates[h]["sidx"] = 0
            s.add("dve", lambda e: e.memset(SST[hp][0][:], 0.0), writes=[("SST", hp, 0)])

        def init_sample(h):
            dma("sp", S0[:, :, :], shg_d[:, h, :, :].rearrange("j k v -> k j v"), writes=STK, semkey="st")
            dma("pool", S0bf[:, :, :], shg_d[:, h, :, :].rearrange("j k v -> k j v"), writes=["S0bf"], semkey="stb")

        load_unit_weights(1)
        prev = None
        init_state(0)
        for bi in range(3):
            prev = run_block(make_hgrn_block(0, False, bi), prev, (lambda: init_sample(0)) if bi == 2 else None)
            if bi < 2:
                for t in (range(0, 5) if bi == 0 else range(5, 9)):
                    load_xT(xm_d[t * 128:(t + 1) * 128, :], xT, 2 + t * 128, ("xT", t))
        for h in range(16):
            nh = h + 1
            prev = run_block(make_hgrn_block(h, True, 0), prev)
            prev = run_block(make_hgrn_block(h, True, 1), prev)
            if nh < 16:
                init_state(nh)
                prev = run_block(make_hgrn_block(nh, False, 0), prev)
            prev = run_block(make_hgrn_block(h, True, 2), prev)
            load_unit_weights(h + 2)
            if nh < 16:
                prev = run_block(make_hgrn_block(nh, False, 1), prev, lambda nh=nh: init_sample(nh))
                prev = run_block(make_hgrn_block(nh, False, 2), prev)
        _interleave([], prev.pieces)
        if prev.fin is not None:
            prev.fin()
        prev = None
        s.add("dve", lambda e: e.memset(dummy[:, 0:1], 0.0),
              writes=["dummy", "S0bf", "VBLK", "UB", "USM", "UO"] + XPK + [("mixa", g) for g in range(16)])
        proj_rot["banks"] = [0, 1, 2, 5, 6, 7]
        for g in range(16):
            u = 16 + g
            if g >= 1 and u + 1 < 32:
                load_unit_weights(u + 1)
            blocks = [make_conv_block(g, bi) for bi in range(3)]
            prev = run_blocks(blocks, prev)
        _interleave([], prev.pieces)

        for g in range(16):
            b = 5 + (g // 4) % 2
            s.add("pe", lambda e, g=g, b=b: e.transpose(pbank[b][0:34, (g % 4) * 128:(g % 4 + 1) * 128],
                                                        UO[:, g, :], identf[:]),
                  reads=["UO", "identf"], writes=[bk(b)])
            if g % 4 == 3:
                gq = g // 4
                s.add("dve", lambda e, gq=gq, b=b: e.tensor_copy(out=CVO[0:34, gq * 512:(gq + 1) * 512],
                                                                 in_=pbank[b][0:34, :]),
                      reads=[bk(b)], writes=STK)
        dma("sp", ncv_d[:, :], CVO[0:34, :], reads=STK, semkey="ncv", final=True)

        s.barrier("dve", lambda e: e.memset(dummy[:, 3:4], 0.0))
        for t in range(9):
            dma("sp", HB[:, t, :], xm_d[t * 128:(t + 1) * 128, :], writes=[("hb", t)], semkey="hb%d" % t)
        dma("sp", LNG[:], lng_d[:, :], writes=["lng"], semkey="c_lng")
        dma("sp", LNB[:], lnb_d[:, :], writes=["lnb"], semkey="c_lnb")

        def load_wo(n8):
            sl = n8 % 2
            for pc in range(4):
                dma("pool", WO[sl][:, pc * 8:(pc + 1) * 8, :], wo_v[:, pc * 8:(pc + 1) * 8, n8 * 256:(n8 + 1) * 256],
                    writes=[("wo", sl, pc)], semkey="wo%d_%d" % (sl, pc))

        stat = T("stat2_s", [128, 16], F32)
        rsum = T("rsum_s", [128, 72], F32)
        s.add("dve", lambda e: e.memset(rsum[:], 0.0), writes=["rsum"])
        JUNK = carve(73728, [128, D_MODEL], F32)
        jkeys = [("wo", 0, pc) for pc in range(4)]

        def ln_stats(t):
            hv = HB[:, t, :]
            p8 = (t % 2) * 8
            c_sum, c_nm, c_ssq, c_rstd = (stat[:, p8 + i:p8 + i + 1] for i in range(4))
            sk = ("stat", t % 2)
            s.add("dve", lambda e: e.memset(stat[:, p8:p8 + 8], 0.0), writes=[sk])
            s.add("dve", lambda e: e.reduce_sum(out=c_sum, in_=rsum[:, t * 8:(t + 1) * 8], axis=mybir.AxisListType.X),
                  reads=[("hb", t), "rsum", sk], writes=[sk])
            s.add("dve", lambda e: e.tensor_scalar(out=c_nm, in0=c_sum, scalar1=-1.0 / D_MODEL, scalar2=None,
                                                   op0=ALU.mult), reads=[sk], writes=[sk])
            s.add("act", lambda e: e.activation(out=hv, in_=hv, func=AF.Identity, bias=c_nm),
                  reads=[("hb", t), sk], writes=[("hb", t)])
            s.add("act", lambda e: e.activation(out=JUNK[:, :], in_=hv, func=AF.Square, accum_out=c_ssq),
                  reads=[("hb", t), sk], writes=[sk] + jkeys)
            s.add("act", lambda e: e.activation(out=c_rstd, in_=c_ssq, func=AF.Ln, scale=1.0 / D_MODEL, bias=EPS),
                  reads=[sk], writes=[sk])
            s.add("act", lambda e: e.activation(out=c_rstd, in_=c_rstd, func=AF.Exp, scale=-0.5),
                  reads=[sk], writes=[sk])

        def ln_apply(t):
            hv = HB[:, t, :]
            p8 = (t % 2) * 8
            c_rstd = stat[:, p8 + 3:p8 + 4]
            sk = ("stat", t % 2)
            s.add("dve", lambda e: e.scalar_tensor_tensor(out=hv, in0=hv, scalar=c_rstd, in1=LNG[:, :],
                                                          op0=ALU.mult, op1=ALU.mult),
                  reads=[sk, "lng", ("hb", t)], writes=[("hb", t)])
            s.add("dve", lambda e: e.tensor_tensor(out=hv, in0=hv, in1=LNB[:, :], op=ALU.add),
                  reads=["lnb", ("hb", t)], writes=[("hb", t)])
            dma("sp", y_d[t * 128:(t + 1) * 128, :], hv, reads=[("hb", t)], semkey="y%d" % t, final=True)

        load_wo(0)
        rot = 0
        banks2 = [0, 1, 2, 4, 5, 6, 7]
        for n8 in range(8):
            if n8 + 1 < 8:
                load_wo(n8 + 1)
            sl = n8 % 2
            for t in range(9):
                b = banks2[rot % len(banks2)]
                rot += 1
                for kc in range(32):
                    mx = MIXA[:, kc, t * 128:(t + 1) * 128] if kc < 16 else MIXB[:, kc - 16, t * 128:(t + 1) * 128]
                    s.add("pe", lambda e, kc=kc, mx=mx, b=b, sl=sl: e.matmul(pbank[b][:, 0:256], mx, WO[sl][:, kc, :],
                                                                             start=(kc == 0), stop=(kc == 31)),
                          reads=[("wo", sl, kc // 8)], writes=[bk(b)])
                hv8 = HB[:, t, n8 * 256:(n8 + 1) * 256]
                rs = rsum[:, t * 8 + n8:t * 8 + n8 + 1]
                s.add("dve", lambda e, hv8=hv8, b=b, rs=rs: e.scalar_tensor_tensor(out=hv8, in0=hv8, scalar=ALPHA,
                                                                                   in1=pbank[b][:, 0:256],
                                                                                   op0=ALU.mult, op1=ALU.add,
                                                                                   accum_out=rs),
                      reads=[bk(b), ("hb", t), "rsum"], writes=[("hb", t)])
                if n8 == 7:
                    ln_stats(t)
                    if t >= 1:
                        ln_apply(t - 1)
        ln_apply(8)

        s.finalize(st)
        with nc.Block() as block:
            @block.tensor
            def _(e):
                s.emit("pe", e)

            @block.scalar
            def _(e):
                s.emit("act", e)

            @block.vector
            def _(e):
                s.emit("dve", e)

            @block.gpsimd
            def _(e):
                s.emit("pool", e)

            @block.sync
            def _(e):
                s.emit("sp", e)
    return nc


_NC_CACHE = {}


def _consts():
    idx = np.arange(128)
    identf = np.eye(128, dtype=np.float32)
    sidx = idx[:, None]
    tidx = idx[None, :]
    maskp = ((sidx <= tidx) & (sidx // 64 == tidx // 64)).astype(np.float32)
    masks = ((sidx <= tidx) & (sidx // 8 == tidx // 8)).astype(np.float32)
    mk64 = np.ones((128, 385), np.float32)
    mk64[:, 0::64] = 0.0
    mkb = np.ones((128, 385), np.float32)
    mkb[:, 0:256:64] = 0.0
    mkb[:, 256::8] = 0.0
    selcol = (idx[:, None] // 8 == np.arange(16)[None, :]).astype(np.float32)
    onesdiv = np.full((128, 128), 1.0 / 128.0, np.float32)
    return dict(identf=identf, maskp=maskp, masks=masks, mk64=mk64, mkb=mkb, selcol=selcol, onesdiv=onesdiv,
                mr64=(1.0 - mk64).astype(np.float32), mrb=(1.0 - mkb).astype(np.float32))


def kernel(x_prompt, x_sample, state_conv, state_hgrn, w_in, conv_w, norm_a, lb_logits, norm_b, w_out,
           ln_gain, ln_bias):
    f32 = np.float32
    x_prompt = np.asarray(x_prompt, f32)
    x_sample = np.asarray(x_sample, f32)
    state_conv = np.asarray(state_conv, f32)
    state_hgrn = np.asarray(state_hgrn, f32)
    w_in = np.asarray(w_in, f32)
    w_out = np.asarray(w_out, f32)
    if "nc" not in _NC_CACHE:
        _NC_CACHE["nc"] = build()
    nc = _NC_CACHE["nc"]

    w8 = w_in[0].reshape(D_MODEL, 8, 16, 128)
    wh = w8[:, 4:8].transpose(0, 2, 1, 3)
    wc = w8[:, 0:4].transpose(0, 2, 1, 3)
    wu = np.ascontiguousarray(np.concatenate([wh, wc], axis=1).reshape(D_MODEL, 32 * 512))
    wo = np.ascontiguousarray(w_out[0])

    def cm(vec):
        return np.ascontiguousarray(np.asarray(vec, f32).reshape(16, 128).T)

    cw = np.ascontiguousarray(np.asarray(conv_w, f32)[0].reshape(3, 16, 128).transpose(2, 1, 0).reshape(128, 48))
    na = cm(norm_a[0])
    nb = cm(norm_b[0])
    lbl = np.ascontiguousarray(np.concatenate([cm(lb_logits[0]), cm(lb_logits[1])], axis=1))
    lng = np.ascontiguousarray(np.broadcast_to(np.asarray(ln_gain, f32)[0][None, :], (128, D_MODEL)))
    lnb = np.ascontiguousarray(np.broadcast_to(np.asarray(ln_bias, f32)[0][None, :], (128, D_MODEL)))
    consts = _consts()

    in_maps = []
    for c in range(NCORES):
        b, hf = c // 2, c % 2
        xm = np.concatenate([x_prompt[b, hf * NPR:(hf + 1) * NPR],
                             x_sample[16 * c:16 * c + 16].reshape(NSM, D_MODEL)], axis=0)
        xp = x_prompt[b, 0:NPR] if hf == 1 else np.zeros((NPR, D_MODEL), f32)
        m = dict(xm=np.ascontiguousarray(xm), xp=np.ascontiguousarray(xp), wu=wu, wo=wo, cw=cw, na=na, nb=nb,
                 lbl=lbl, lng=lng, lnb=lnb,
                 scv=np.ascontiguousarray(state_conv[0, 16 * c:16 * c + 16].reshape(32, D_MODEL)),
                 shg=np.ascontiguousarray(state_hgrn[0, 16 * c:16 * c + 16]))
        m.update(consts)
        in_maps.append(m)

    res = run_bass_kernel_spmd(nc, in_maps, core_ids=list(range(NCORES)))
    R = res.results

    y_prompt = np.empty((4, 2048, D_MODEL), f32)
    y_sample = np.empty((128, 8, D_MODEL), f32)
    ncp = np.empty((1, 4, 2, 2048), f32)
    nhp = np.empty((1, 4, 16, 128, 128), f32)
    ncs = np.empty((1, 128, 2, 2048), f32)
    nhs = np.empty((1, 128, 16, 128, 128), f32)
    for c in range(NCORES):
        b, hf = c // 2, c % 2
        r = R[c]
        y_prompt[b, hf * NPR:(hf + 1) * NPR] = r["y"][0:NPR]
        y_sample[16 * c:16 * c + 16] = r["y"][NPR:NTOK].reshape(16, 8, D_MODEL)
        ncs[0, 16 * c:16 * c + 16] = r["ncv"][2:34].reshape(16, 2, 2048)
        nhs[0, 16 * c:16 * c + 16] = r["nhs"]
        if hf == 1:
            ncp[0, b] = r["ncv"][0:2]
            nhp[0, b] = r["nhp"]
    return (y_prompt, y_sample, ncp, nhp, ncs, nhs)
```
